# Optimizing a Trainium2 kernel written in Bass

```python
import jax, jax.numpy as jnp
from jax import lax
import numpy as np

D_MODEL = 1024
BATCH = 16
SEQ = 4096
DEPTH = 4

N_A_LAYERS = DEPTH // 2
N_B_LAYERS = DEPTH - N_A_LAYERS
POOL_WINDOWS = (2, 4, 8, 16)
N_POOL_GROUPS = len(POOL_WINDOWS)
POOL_GROUP_DIM = D_MODEL // N_POOL_GROUPS
QK_NOPE_DIM = 128
QK_ROPE_DIM = 64
V_HEAD_DIM = 128
N_HEADS = D_MODEL // 128
Q_LORA_RANK = D_MODEL // 2
KV_LORA_RANK = D_MODEL // 4
ROPE_THETA = 10000.0
D_FF = 4 * D_MODEL
Q_BLOCK = 128
N_MOD = 6
DEEPNORM_ALPHA = (2.0 * DEPTH) ** 0.25
DEEPNORM_BETA = (8.0 * DEPTH) ** -0.25
LN_EPS = 1e-5
RMS_EPS = 1e-6
MAX_START = 1024
ATTN_SCALE = (QK_NOPE_DIM + QK_ROPE_DIM) ** -0.5

kernel_name = "yoco_pool_mla_deepnorm_adaln"


def layer_norm(x, g, b):
    xf = x.astype(jnp.float32)
    mu = jnp.mean(xf, axis=-1, keepdims=True)
    xc = xf - mu
    var = jnp.mean(xc * xc, axis=-1, keepdims=True)
    return (xc * lax.rsqrt(var + LN_EPS) * g + b).astype(x.dtype)


def rms_norm(x, g):
    xf = x.astype(jnp.float32)
    ms = jnp.mean(xf * xf, axis=-1, keepdims=True)
    return (xf * lax.rsqrt(ms + RMS_EPS) * g).astype(x.dtype)


def rope(x, cos, sin):
    x1, x2 = jnp.split(x, 2, axis=-1)
    return jnp.concatenate([x1 * cos - x2 * sin, x2 * cos + x1 * sin], axis=-1)


def modulate(x, shift, scale):
    return x * (1.0 + scale[:, None, :]) + shift[:, None, :]


def causal_multiscale_pool(h):
    b, s, d = h.shape
    hf = h.astype(jnp.float32)
    cs = jnp.cumsum(hf, axis=1)
    t = jnp.arange(s)
    outs = []
    for g, w in enumerate(POOL_WINDOWS):
        csg = cs[..., g * POOL_GROUP_DIM:(g + 1) * POOL_GROUP_DIM]
        shifted = jnp.pad(csg[:, :s - w], ((0, 0), (w, 0), (0, 0)))
        cnt = jnp.minimum(t + 1, w).astype(jnp.float32)[None, :, None]
        outs.append((csg - shifted) / cnt)
    pooled = jnp.concatenate(outs, axis=-1)
    return (pooled - hf).astype(h.dtype)


def pool_mixer(h, w_pool, scale):
    b, s, d = h.shape
    y = causal_multiscale_pool(h).reshape(b, s, N_POOL_GROUPS, POOL_GROUP_DIM)
    y = jnp.einsum('bsgc,gcd->bsgd', y, w_pool).reshape(b, s, d)
    return y * scale


def sq_relu_mlp(h, w1, w2):
    a = jax.nn.relu(h @ w1)
    return (a * a) @ w2


def shared_kv(x, kv_in_w, kv_norm_g, k_up_w, v_up_w, cos, sin):
    b, s, _ = x.shape
    ckr = x @ kv_in_w
    c_kv = rms_norm(ckr[..., :KV_LORA_RANK], kv_norm_g)
    k_rope = rope(ckr[..., KV_LORA_RANK:], cos, sin)
    k_nope = (c_kv @ k_up_w).reshape(b, s, N_HEADS, QK_NOPE_DIM)
    v = (c_kv @ v_up_w).reshape(b, s, N_HEADS, V_HEAD_DIM)
    return k_nope, k_rope, v


def mla_attention(h, q_down_w, q_norm_g, q_up_w, out_w, k_nope, k_rope, v, cos, sin):
    b, s, _ = h.shape
    cq = rms_norm(h @ q_down_w, q_norm_g)
    q = (cq @ q_up_w).reshape(b, s, N_HEADS, QK_NOPE_DIM + QK_ROPE_DIM)
    q_nope = q[..., :QK_NOPE_DIM]
    q_rope = rope(q[..., QK_NOPE_DIM:], cos[:, :, None, :], sin[:, :, None, :])
    nb = s // Q_BLOCK
    qn = q_nope.reshape(b, nb, Q_BLOCK, N_HEADS, QK_NOPE_DIM).transpose(1, 0, 2, 3, 4)
    qr = q_rope.reshape(b, nb, Q_BLOCK, N_HEADS, QK_ROPE_DIM).transpose(1, 0, 2, 3, 4)
    kpos = jnp.arange(s)

    def one_block(args):
        qn_b, qr_b, i = args
        sc = (jnp.einsum('bqhd,bkhd->bhqk', qn_b, k_nope)
              + jnp.einsum('bqhr,bkr->bhqk', qr_b, k_rope)).astype(jnp.float32) * ATTN_SCALE
        qpos = i * Q_BLOCK + jnp.arange(Q_BLOCK)
        mask = kpos[None, :] <= qpos[:, None]
        sc = jnp.where(mask[None, None], sc, -jnp.inf)
        p = jax.nn.softmax(sc, axis=-1).astype(v.dtype)
        return jnp.einsum('bhqk,bkhd->bqhd', p, v)

    o = lax.map(one_block, (qn, qr, jnp.arange(nb)))
    o = o.transpose(1, 0, 2, 3, 4).reshape(b, s, N_HEADS * V_HEAD_DIM)
    return o @ out_w


def setup_inputs(seed: int = 0) -> dict:
    key = jax.random.key(seed)
    ks = jax.random.split(key, 20)
    n = jax.random.normal
    f32 = jnp.float32
    D = D_MODEL
    qk_dim = QK_NOPE_DIM + QK_ROPE_DIM
    start = jax.random.randint(ks[2], (BATCH, 1), 0, MAX_START, dtype=jnp.int32)
    positions = (start + jnp.arange(SEQ, dtype=jnp.int32)[None, :]).astype(jnp.int32)
    return {
        "x": n(ks[0], (BATCH, SEQ, D), f32),
        "c": n(ks[1], (BATCH, D), f32),
        "positions": positions,
        "ada_w": n(ks[3], (DEPTH, D, N_MOD * D), f32) * (0.5 * D ** -0.5),
        "ada_b": 0.01 * n(ks[4], (DEPTH, N_MOD * D), f32),
        "ln_g": 1.0 + 0.02 * n(ks[5], (DEPTH, 2, D), f32),
        "ln_b": 0.02 * n(ks[6], (DEPTH, 2, D), f32),
        "mlp_w1": n(ks[7], (DEPTH, D, D_FF), f32) * D ** -0.5,
        "mlp_w2": n(ks[8], (DEPTH, D_FF, D), f32) * (D_FF ** -0.5 * DEEPNORM_BETA),
        "pool_w": n(ks[9], (N_A_LAYERS, N_POOL_GROUPS, POOL_GROUP_DIM, POOL_GROUP_DIM), f32)
                  * (POOL_GROUP_DIM ** -0.5 * DEEPNORM_BETA),
        "pool_scale": 1.0 + 0.1 * n(ks[10], (N_A_LAYERS, D), f32),
        "q_down_w": n(ks[11], (N_B_LAYERS, D, Q_LORA_RANK), f32) * D ** -0.5,
        "q_norm_g": 1.0 + 0.02 * n(ks[12], (N_B_LAYERS, Q_LORA_RANK), f32),
        "q_up_w": n(ks[13], (N_B_LAYERS, Q_LORA_RANK, N_HEADS * qk_dim), f32) * Q_LORA_RANK ** -0.5,
        "attn_out_w": n(ks[14], (N_B_LAYERS, N_HEADS * V_HEAD_DIM, D), f32)
                      * ((N_HEADS * V_HEAD_DIM) ** -0.5 * DEEPNORM_BETA),
        "kv_in_w": n(ks[15], (D, KV_LORA_RANK + QK_ROPE_DIM), f32) * D ** -0.5,
        "kv_norm_g": 1.0 + 0.02 * n(ks[16], (KV_LORA_RANK,), f32),
        "k_up_w": n(ks[17], (KV_LORA_RANK, N_HEADS * QK_NOPE_DIM), f32) * KV_LORA_RANK ** -0.5,
        "v_up_w": n(ks[18], (KV_LORA_RANK, N_HEADS * V_HEAD_DIM), f32)
                  * (KV_LORA_RANK ** -0.5 * DEEPNORM_BETA),
    }


def reference(x, c, positions, ada_w, ada_b, ln_g, ln_b, mlp_w1, mlp_w2, pool_w, pool_scale,
              q_down_w, q_norm_g, q_up_w, attn_out_w, kv_in_w, kv_norm_g, k_up_w, v_up_w):
    b, s, d = x.shape
    mod = (jnp.einsum('bd,ldm->blm', jax.nn.silu(c), ada_w) + ada_b[None]).reshape(b, DEPTH, N_MOD, d)
    inv_freq = ROPE_THETA ** (-jnp.arange(0, QK_ROPE_DIM, 2, dtype=jnp.float32) / QK_ROPE_DIM)
    ang = positions.astype(jnp.float32)[..., None] * inv_freq
    cos = jnp.cos(ang).astype(x.dtype)
    sin = jnp.sin(ang).astype(x.dtype)

    for l in range(DEPTH):
        shift1, scale1, gate1 = mod[:, l, 0], mod[:, l, 1], mod[:, l, 2]
        shift2, scale2, gate2 = mod[:, l, 3], mod[:, l, 4], mod[:, l, 5]
        h = modulate(x, shift1, scale1)
        if l < N_A_LAYERS:
            y = pool_mixer(h, pool_w[l], pool_scale[l])
        else:
            if l == N_A_LAYERS:
                k_nope, k_rope, v = shared_kv(x, kv_in_w, kv_norm_g, k_up_w, v_up_w, cos, sin)
            j = l - N_A_LAYERS
            y = mla_attention(h, q_down_w[j], q_norm_g[j], q_up_w[j], attn_out_w[j],
                              k_nope, k_rope, v, cos, sin)
        x = layer_norm(DEEPNORM_ALPHA * x + gate1[:, None, :] * y, ln_g[l, 0], ln_b[l, 0])
        h = modulate(x, shift2, scale2)
        y = sq_relu_mlp(h, mlp_w1[l], mlp_w2[l])
        x = layer_norm(DEEPNORM_ALPHA * x + gate2[:, None, :] * y, ln_g[l, 1], ln_b[l, 1])
    return x
```

```python
import math
from contextlib import ExitStack

import numpy as np
import ml_dtypes
import concourse.bass as bass
import concourse.mybir as mybir
from concourse.bass_utils import run_bass_kernel_spmd

F32 = mybir.dt.float32
BF16 = mybir.dt.bfloat16
I32 = mybir.dt.int32
ALU = mybir.AluOpType
AF = mybir.ActivationFunctionType

D = 1024
NCH = 8
DEPTH = 4
NA = 2
DFF = 4096
NH = 8
QR = 512
KVR = 256
T = 512
ALPHA = (2.0 * DEPTH) ** 0.25
LN_EPS = 1e-5
RMS_EPS = 1e-6
ATTN_SCALE = (128 + 64) ** -0.5
N_WT_POOL = 17
N_WT_ATT = 21
TWO_PI = 2.0 * math.pi
C1 = 6.28125
C2 = TWO_PI - C1


class Ctr:
    def __init__(self, sem, name):
        self.sem = sem
        self.name = name
        self.count = 0


class Buf:
    __slots__ = ("name", "lw", "rd")

    def __init__(self, name):
        self.name = name
        self.lw = None
        self.rd = {}


class Eng:
    def __init__(self, name, h, ctr, is_pe=False):
        self.name = name
        self.h = h
        self.ctr = ctr
        self.waited = {}
        self.is_pe = is_pe


def _flat(lst):
    out = []
    for b in lst:
        if isinstance(b, (list, tuple)):
            out.extend(_flat(b))
        else:
            out.append(b)
    return out


class Prog:
    def __init__(self, nc, es):
        self.nc = nc
        self.es = es
        self.nsem = 0
        self.stopped = False
        self.log = {}
        self.pend = {}

        def mk(name, h, is_pe=False):
            return Eng(name, h, self.new_ctr(name), is_pe)

        self.pe = mk("pe", nc.tensor, True)
        self.act = mk("act", nc.scalar)
        self.dve = mk("dve", nc.vector)
        self.pool = mk("pool", nc.gpsimd)
        self.sp = mk("sp", nc.sync)

    def new_ctr(self, name):
        sem = self.es.enter_context(self.nc.semaphore("s_" + name))
        self.nsem += 1
        return Ctr(sem, name)

    def _deps(self, eng, reads, writes, own_ctr):
        deps = {}

        def add(c, v):
            if deps.get(c, 0) < v:
                deps[c] = v

        for b in reads:
            if b.lw is not None:
                add(*b.lw)
        for b in writes:
            if b.lw is not None:
                add(*b.lw)
            for c, v in b.rd.items():
                add(c, v)
        for c, v in deps.items():
            if c is own_ctr and eng.is_pe:
                continue
            if eng.waited.get(c, 0) >= v:
                continue
            if c is own_ctr and v > c.count:
                continue
            eng.h.wait_ge(c.sem, v)
            eng.waited[c] = v

    def op(self, eng, fn, reads=(), writes=(), signal=True):
        if self.stopped:
            return None
        c = eng.ctr
        reads = _flat(reads)
        writes = _flat(writes)
        self._deps_compute(eng, reads, writes)
        ins = fn()
        self.log.setdefault(eng.name, []).append((self.pend.pop(eng.name, []), c if signal else None, 1, "op"))
        if signal:
            c.count += 1
            ins.then_inc(c.sem, 1)
            v = c.count
        else:
            v = c.count + 1
        for b in writes:
            b.lw = (c, v)
            b.rd = {}
        for b in reads:
            if b.rd.get(c, 0) < v:
                b.rd[c] = v
        return ins

    def _deps_compute(self, eng, reads, writes):
        own = eng.ctr
        deps = {}

        def add(c, v):
            if deps.get(c, 0) < v:
                deps[c] = v

        for b in reads:
            if b.lw is not None:
                add(*b.lw)
        for b in writes:
            if b.lw is not None and b.lw[0] is not own:
                add(*b.lw)
            for c, v in b.rd.items():
                if c is not own:
                    add(c, v)
        for c, v in deps.items():
            if c is own:
                if eng.is_pe:
                    continue
                if v > c.count:
                    continue
            if eng.waited.get(c, 0) >= v:
                continue
            eng.h.wait_ge(c.sem, v)
            eng.waited[c] = v
            self.pend.setdefault(eng.name, []).append((c, v))

    def dma(self, eng, ctr, out, in_, reads=(), writes=(), **kw):
        if self.stopped:
            return None
        reads = _flat(reads)
        writes = _flat(writes)
        deps = {}

        def add(c, v):
            if deps.get(c, 0) < v:
                deps[c] = v

        for b in reads:
            if b.lw is not None:
                add(*b.lw)
        for b in writes:
            if b.lw is not None:
                add(*b.lw)
            for c, v in b.rd.items():
                add(c, v)
        for c, v in deps.items():
            if eng.waited.get(c, 0) >= v:
                continue
            eng.h.wait_ge(c.sem, v)
            eng.waited[c] = v
            self.pend.setdefault(eng.name, []).append((c, v))
        ins = eng.h.dma_start(out=out, in_=in_, **kw)
        self.log.setdefault(eng.name, []).append((self.pend.pop(eng.name, []), ctr, 16, "dma"))
        ctr.count += 16
        ins.then_inc(ctr.sem, 16)
        v = ctr.count
        for b in writes:
            b.lw = (ctr, v)
            b.rd = {}
        for b in reads:
            if b.rd.get(ctr, 0) < v:
                b.rd[ctr] = v
        return ins

    def check_deadlock(self):
        val = {}
        pos = {k: 0 for k in self.log}
        progress = True
        while progress:
            progress = False
            for k, lst in self.log.items():
                while pos[k] < len(lst):
                    waits, c, inc, kind = lst[pos[k]]
                    if all(val.get(cc, 0) >= vv for cc, vv in waits):
                        if c is not None:
                            val[c] = val.get(c, 0) + inc
                        pos[k] += 1
                        progress = True
                    else:
                        break
        stuck = {k: pos[k] for k in self.log if pos[k] < len(self.log[k])}
        for k, p in stuck.items():
            waits, c, inc, kind = self.log[k][p]
            print("DEADLOCK: engine", k, "instr", p, "/", len(self.log[k]), kind, "waits",
                  [(cc.name, vv, val.get(cc, 0)) for cc, vv in waits])
        return not stuck

    def wait_all(self, eng, ctrs):
        for c in ctrs:
            if c.count > 0 and eng.waited.get(c, 0) < c.count:
                eng.h.wait_ge(c.sem, c.count)
                eng.waited[c] = c.count


def _wtile_from_rows(w, kcs, cols):
    K = w.shape[0]
    assert K == kcs * 128
    sub = w[:, cols]
    return np.ascontiguousarray(sub.reshape(kcs, 128, len(cols)).transpose(1, 0, 2).reshape(128, -1))


def build_weight_image(inp):
    tiles = []

    def pad(t):
        out = np.zeros((128, 4096), np.float32)
        out[:, : t.shape[1]] = t
        return out

    def mlp_tiles(l):
        w1 = inp["mlp_w1"][l]
        w2 = inp["mlp_w2"][l]
        for g in range(8):
            tiles.append(pad(_wtile_from_rows(w1, 8, np.arange(g * 512, (g + 1) * 512))))
        for half in range(2):
            for jt in range(4):
                rows = w2[jt * 1024:(jt + 1) * 1024, half * 512:(half + 1) * 512]
                tiles.append(pad(_wtile_from_rows(rows, 8, np.arange(512))))

    rope_perm = np.concatenate([np.arange(32, 64), np.arange(0, 32)])
    for l in range(DEPTH):
        if l < NA:
            pw = inp["pool_w"][l]
            t = np.concatenate([_wtile_from_rows(pw[g], 2, np.arange(256)) for g in range(4)], axis=1)
            tiles.append(pad(t))
        else:
            j = l - NA
            tiles.append(pad(_wtile_from_rows(inp["q_down_w"][j], 8, np.arange(512))))
            qu = inp["q_up_w"][j]
            nope_cols = np.concatenate([np.arange(h * 192, h * 192 + 128) for h in range(NH)])
            tiles.append(pad(_wtile_from_rows(qu, 4, nope_cols)))
            rope_cols = np.concatenate([np.arange(h * 192 + 128, h * 192 + 192) for h in range(NH)])
            rot_cols = np.concatenate([h * 192 + 128 + rope_perm for h in range(NH)])
            tiles.append(pad(_wtile_from_rows(qu, 4, np.concatenate([rope_cols, rot_cols]))))
            wo = inp["attn_out_w"][j]
            for half in range(2):
                tiles.append(pad(_wtile_from_rows(wo, 8, np.arange(half * 512, (half + 1) * 512))))
        mlp_tiles(l)
    kvw = inp["kv_in_w"]
    kr = np.arange(256, 320)
    kcols = np.concatenate([np.arange(256), kr, kr, 256 + rope_perm, 256 + rope_perm])
    tiles.append(pad(_wtile_from_rows(kvw, 8, kcols)))
    t = np.concatenate([_wtile_from_rows(inp["k_up_w"], 2, np.arange(1024)),
                        _wtile_from_rows(inp["v_up_w"], 2, np.arange(1024))], axis=1)
    tiles.append(pad(t))
    return np.stack(tiles, 0)


def wt_index(l, which, i=0):
    base = 0
    for ll in range(l):
        base += N_WT_POOL if ll < NA else N_WT_ATT
    if which == "kvin":
        return NA * N_WT_POOL + (DEPTH - NA) * N_WT_ATT
    if which == "kvup":
        return NA * N_WT_POOL + (DEPTH - NA) * N_WT_ATT + 1
    if l < NA:
        off = {"pool": 0, "w1": 1, "w2": 9}[which]
    else:
        off = {"qd": 0, "quA": 1, "quB": 2, "wo": 3, "w1": 5, "w2": 13}[which]
    return base + off + i


NT_W = NA * N_WT_POOL + (DEPTH - NA) * N_WT_ATT + 2


def fm(v):
    v = np.asarray(v, np.float32)
    lead = v.shape[:-1]
    n = v.shape[-1] // 128
    r = v.reshape(*lead, n, 128)
    return np.ascontiguousarray(np.moveaxis(r, -1, 0))


def build_ada_image(ada_w):
    a = ada_w.reshape(DEPTH, 8, 128, 12, 512)
    return np.ascontiguousarray(a.transpose(0, 3, 2, 1, 4))


class _Stop(Exception):
    pass


def build_program(NSEQ, S, stop=None):
    NTILE = S // T
    NKB = S // T
    nc = bass.Bass("TRN2", target_bir_lowering=False)
    es = ExitStack()

    def din(name, shape, dt=F32):
        return nc.dram_tensor(name, list(shape), dt, kind="ExternalInput").ap()

    xT = din("xT", [NSEQ, D, S])
    pos_t = nc.dram_tensor("pos", [NSEQ, S], I32, kind="ExternalInput")
    cT_d = din("cT", [128, NCH, NSEQ])
    adaw_d = din("adaw", [DEPTH, 12, 128, 8, 512])
    adab_d = din("adab", [128, DEPTH, 48])
    lng_d = din("lng", [128, DEPTH, 2, NCH])
    lnb_d = din("lnb", [128, DEPTH, 2, NCH])
    psc_d = din("psc", [128, NA, NCH])
    qng_d = din("qng", [128, DEPTH - NA, 4])
    kvg_d = din("kvg", [128, 2])
    cst_d = din("cst", [128, 32])
    tri_d = din("tri", [128, 128], BF16)
    wimg32 = din("wimg32", [NT_W, 128, 4096])
    outT = nc.dram_tensor("outT", [NSEQ, D, S], F32, kind="ExternalOutput").ap()
    dbg_d = nc.dram_tensor("dbg", [128, 4096], F32, kind="ExternalOutput").ap() if stop else None
    wimg = nc.dram_tensor("wimg", [NT_W, 128, 4096], BF16, kind="Internal").ap()
    kT_d = nc.dram_tensor("kT_d", [NSEQ, NH, 128, S], BF16, kind="Internal").ap()
    v_d = nc.dram_tensor("v_d", [NSEQ, NH, 128, NKB * 4, 128], BF16, kind="Internal").ap()

    P = Prog(nc, es)
    pe, act, dve, pool, sp = P.pe, P.act, P.dve, P.pool, P.sp

    def sb(name, shape, dt):
        return es.enter_context(nc.sbuf_tensor(name, list(shape), dt))

    xin = sb("xin", [128, NCH, T], F32)
    X = sb("X", [128, NCH, T], F32)
    U = sb("U", [128, NCH, T], F32)
    H = sb("H", [128, NCH, T], BF16)
    A = sb("A", [128, 32, T], BF16)
    WR_N = 5
    WR = sb("WR", [128, WR_N, 4096], BF16)
    UB = sb("UB", [128, 3, T], BF16)
    USQ = sb("USQ", [128, 3, T], BF16)
    RT = sb("RT", [128, 3, T], F32)
    ST = sb("ST", [128, 6, T], F32)
    TR = sb("TR", [128, 4, T], F32)
    CS = sb("CS", [128, 2, T], F32)
    POSI = sb("POSI", [128, T], I32)
    KI = POSI
    QRz = sb("QRz", [128, NH, T], BF16)
    CQN = sb("CQN", [128, 4, T], BF16)
    CKVN = sb("CKVN", [128, 2, T], BF16)
    KRS = sb("KRS", [128, S], BF16)
    KB_N = 4
    KBLK = sb("KBLK", [128, KB_N, T], BF16)
    VBLK = sb("VBLK", [128, KB_N, 4, 128], BF16)
    PT = sb("PT", [128, 3, T], BF16)
    HALO = sb("HALO", [128, NA, NCH, 16], F32)
    MOD = sb("MOD", [128, DEPTH, 48, NSEQ], F32)
    DRV = sb("DRV", [128, DEPTH, 2, 5, NCH, NSEQ], F32)
    H0C = sb("H0C", [128, 2, NCH, NSEQ], F32)
    LNG = sb("LNG", [128, DEPTH, 2, NCH], F32)
    LNB = sb("LNB", [128, DEPTH, 2, NCH], F32)
    PSC = sb("PSC", [128, NA, NCH], F32)
    QNG = sb("QNG", [128, DEPTH - NA, 4], F32)
    KVG = sb("KVG", [128, 2], F32)
    CST = sb("CST", [128, 32], F32)
    ADAB = sb("ADAB", [128, DEPTH, 48], F32)
    CTs = sb("CTs", [128, NCH, NSEQ], F32)
    SC = sb("SC", [128, NCH, NSEQ], F32)
    TRI = sb("TRI", [128, 128], BF16)
    ONES = sb("ONES", [128, 128], BF16)
    TMPS = sb("TMPS", [128, 4, NCH], F32)

    Aflat = A[:, :, :]
    QT = A
    A32 = A[:, :, :].rearrange("p c t -> p (c t)").bitcast(F32)
    ADAT = A32.rearrange("p (s k c) -> p s k c", s=2, k=8)
    HW = A32[:, 0:NCH * 528].rearrange("p (m t) -> p m t", t=528)
    PTMP = A32[:, 17 * 256:17 * 256 + 4 * 528].rearrange("p (k t) -> p k t", t=528)
    C32 = A32[:, 0:4 * T].rearrange("p (m t) -> p m t", t=T)

    PS = [es.enter_context(nc.psum_tensor(f"ps{i}", [128, T], F32)) for i in range(8)]
    psb = [Buf(f"ps{i}") for i in range(8)]
    rot_state = [0]

    def rot():
        i = rot_state[0] % 4
        rot_state[0] += 1
        return i

    def bl(name, n):
        return [Buf(f"{name}{i}") for i in range(n)]

    b_xin, b_X, b_U, b_H = bl("xin", NCH), bl("X", NCH), bl("U", NCH), bl("H", NCH)
    b_A = bl("A", 32)
    b_WR = bl("WR", WR_N)
    b_UB, b_USQ, b_RT = bl("UB", 3), bl("USQ", 3), bl("RT", 3)
    b_ST, b_TR = bl("ST", 6), bl("TR", 4)
    b_CS = Buf("CS")
    b_POSI = Buf("POSI")
    b_KI = b_POSI
    b_QR, b_CQN, b_CKVN = bl("QR", NH), bl("CQN", 4), bl("CKVN", 2)
    b_ones = Buf("ones")
    b_KRS = bl("KRS", NKB)
    b_KBLK, b_VBLK = bl("KBLK", KB_N), bl("VBLK", KB_N)
    b_PT = bl("PT", 3)
    b_HALO = [bl(f"HALO{l}_", NCH) for l in range(NA)]
    def a_cover(b0, b1):
        return b_A[b0 // 1024:(b1 - 1) // 1024 + 1]

    b_PTMP = [a_cover(17408 + k * 2112, 17408 + (k + 1) * 2112) for k in range(4)]
    b_HW = [a_cover(m * 2112, (m + 1) * 2112) for m in range(NCH)]
    b_C32 = [b_A[2 * m:2 * m + 2] for m in range(4)]
    b_ADAT = [b_A[0:16], b_A[16:32]]
    b_const = Buf("const")
    b_MOD = Buf("MOD")
    b_wimg = Buf("wimg")
    b_kd = [bl(f"kd{s}_", NKB) for s in range(NSEQ)]
    b_vd = [bl(f"vd{s}_", NKB) for s in range(NSEQ)]
    b_out = Buf("out")
    b_TMPS = Buf("TMPS")

    c_const = P.new_ctr("const")
    c_cvt = P.new_ctr("cvt")
    c_ada = [P.new_ctr(f"ada{i}") for i in range(2)]
    c_wr = [P.new_ctr(f"wr{i}") for i in range(WR_N)]
    c_xin = P.new_ctr("xin")
    c_out = P.new_ctr("out")
    c_kst = P.new_ctr("kst")
    c_vst = P.new_ctr("vst")
    c_kb = [P.new_ctr(f"kb{i}") for i in range(KB_N)]
    c_vb = [P.new_ctr(f"vb{i}") for i in range(KB_N)]
    c_pos = P.new_ctr("pos")

    c_dbg = P.new_ctr("dbg")

    def checkpoint(label, ap=None, bufs=()):
        if stop != label:
            return
        if ap is not None:
            n = ap.shape[1]
            P.dma(sp, c_dbg, dbg_d[:, 0:n], ap, reads=list(bufs))
        P.stopped = True

    w32v = wimg32.rearrange("n p (a c) -> (n p a) c", c=2048)
    wbv = wimg.rearrange("n p (a c) -> (n p a) c", c=2048)
    rows_total = NT_W * 128 * 2
    RCH = 1024
    for r0 in range(0, rows_total, RCH):
        r1 = min(rows_total, r0 + RCH)
        P.dma(pool, c_cvt, wbv[r0:r1, :], w32v[r0:r1, :], writes=[b_wimg])
    b_wimg.lw = (c_cvt, c_cvt.count)

    for dst, src in ((CTs, cT_d), (ADAB, adab_d), (LNG, lng_d), (LNB, lnb_d), (PSC, psc_d),
                     (QNG, qng_d), (KVG, kvg_d), (CST, cst_d), (TRI, tri_d)):
        nd = len(dst.shape)
        sl = tuple([slice(None)] * nd)
        P.dma(sp, c_const, dst[sl], src[sl], writes=[b_const])
    b_const.lw = (c_const, c_const.count)
    P.op(dve, lambda: nc.vector.memset(ONES[:, :], 1.0), writes=[b_ones])
    P.op(pool, lambda: nc.gpsimd.memset(QRz[:, :, :], 0.0), writes=b_QR)
    P.op(dve, lambda: nc.vector.memset(HALO[:, :, :, :], 0.0), writes=[b for l in b_HALO for b in l])
    P.op(act, lambda: nc.scalar.activation(out=SC[:, :, :], in_=CTs[:, :, :], func=AF.Silu),
         reads=[b_const], writes=[b_MOD])

    for l in range(DEPTH):
        bank = 4 + (l % 2)
        for g in range(12):
            slot = (l * 12 + g) % 2
            P.dma(sp, c_ada[slot], ADAT[:, slot, :, :], adaw_d[l, g, :, :, :], writes=[b_ADAT[slot]])
            for j in range(4):
                m = g * 4 + j
                for kc in range(8):
                    P.op(pe, lambda: nc.tensor.matmul(
                        PS[bank][:, m * NSEQ:(m + 1) * NSEQ], lhsT=ADAT[:, slot, kc, j * 128:(j + 1) * 128],
                        rhs=SC[:, kc, :], start=(kc == 0), stop=(kc == 7)),
                        reads=[b_ADAT[slot], b_MOD], writes=[psb[bank]], signal=(kc == 7))
        for s in range(NSEQ):
            pv = PS[bank][:, 0:48 * NSEQ].rearrange("p (m s) -> p m s", s=NSEQ)
            P.op(dve, lambda: nc.vector.tensor_tensor(out=MOD[:, l, :, s], in0=pv[:, :, s], in1=ADAB[:, l, :],
                                                      op=ALU.add),
                 reads=[psb[bank], b_const], writes=[b_MOD])

    checkpoint("cvt")
    checkpoint("mod", MOD[:, :, :, :].rearrange("p l m s -> p (l m s)"), [b_MOD])
    def modv(l, j, k, s):
        o = (j * 3 + k) * NCH
        return MOD[:, l, o:o + NCH, s]

    for s in range(NSEQ):
        P.op(dve, lambda: nc.vector.tensor_scalar(out=H0C[:, 0, :, s], in0=modv(0, 0, 1, s), scalar1=1.0,
                                                  scalar2=None, op0=ALU.add), reads=[b_MOD], writes=[b_MOD])
        P.op(dve, lambda: nc.vector.tensor_copy(out=H0C[:, 1, :, s], in_=modv(0, 0, 0, s)),
             reads=[b_MOD], writes=[b_MOD])
        for l in range(DEPTH):
            for j in range(2):
                if j == 0 and l < NA:
                    P.op(dve, lambda: nc.vector.scalar_tensor_tensor(
                        out=DRV[:, l, j, 1, :, s], in0=modv(l, j, 2, s), scalar=1.0 / ALPHA, in1=PSC[:, l, :],
                        op0=ALU.mult, op1=ALU.mult), reads=[b_MOD, b_const], writes=[b_MOD])
                else:
                    P.op(dve, lambda: nc.vector.tensor_scalar(
                        out=DRV[:, l, j, 1, :, s], in0=modv(l, j, 2, s), scalar1=1.0 / ALPHA, scalar2=None,
                        op0=ALU.mult), reads=[b_MOD], writes=[b_MOD])
                if j == 0:
                    l2, j2 = l, 1
                elif l + 1 < DEPTH:
                    l2, j2 = l + 1, 0
                else:
                    l2 = None
                if l2 is not None:
                    P.op(dve, lambda: nc.vector.tensor_scalar(
                        out=DRV[:, l, j, 0, :, s], in0=modv(l2, j2, 1, s), scalar1=1.0, scalar2=None, op0=ALU.add),
                        reads=[b_MOD], writes=[b_MOD])
                    P.op(dve, lambda: nc.vector.tensor_tensor(
                        out=DRV[:, l, j, 2, :, s], in0=LNG[:, l, j, :], in1=DRV[:, l, j, 0, :, s], op=ALU.mult),
                        reads=[b_MOD, b_const], writes=[b_MOD])
                    P.op(dve, lambda: nc.vector.tensor_tensor(
                        out=DRV[:, l, j, 4, :, s], in0=LNB[:, l, j, :], in1=DRV[:, l, j, 0, :, s], op=ALU.mult),
                        reads=[b_MOD, b_const], writes=[b_MOD])
                    P.op(dve, lambda: nc.vector.tensor_tensor(
                        out=DRV[:, l, j, 3, :, s], in0=DRV[:, l, j, 4, :, s], in1=modv(l2, j2, 0, s), op=ALU.add),
                        reads=[b_MOD], writes=[b_MOD])

    checkpoint("drv", DRV[:, :, :, :, :, :].rearrange("p l j k m s -> p (l j k m s)"), [b_MOD])
    wr_state = [0]

    def wload(idx):
        slot = wr_state[0] % WR_N
        wr_state[0] += 1
        P.dma(sp, c_wr[slot], WR[:, slot, :], wimg[idx, :, :], reads=[b_wimg], writes=[b_WR[slot]])
        return slot

    def mm(bank, col0, col1, lhsT, rhs, start, stop, reads, signal=None):
        if signal is None:
            signal = stop
        P.op(pe, lambda: nc.tensor.matmul(PS[bank][:, col0:col1], lhsT=lhsT, rhs=rhs, start=start, stop=stop),
             reads=reads, writes=[psb[bank]], signal=signal)

    ubi = [0]

    def epilogue_chunk(l, j, s, m, bank, xsrc, xsrc_b):
        P.op(dve, lambda: nc.vector.scalar_tensor_tensor(
            out=U[:, m, :], in0=PS[bank][:, :], scalar=DRV[:, l, j, 1, m:m + 1, s], in1=xsrc[:, m, :],
            op0=ALU.mult, op1=ALU.add), reads=[psb[bank], xsrc_b[m], b_MOD], writes=[b_U[m]])

    def stats_chunk(m, first, last):
        i = ubi[0] % 3
        ubi[0] += 1
        P.op(act, lambda: nc.scalar.activation(out=USQ[:, i, :], in_=U[:, m, :], func=AF.Square),
             reads=[b_U[m]], writes=[b_USQ[i]])
        P.op(pool, lambda: nc.gpsimd.tensor_copy(out=UB[:, i, :], in_=U[:, m, :]),
             reads=[b_U[m]], writes=[b_UB[i]])
        mm(4, 0, T, ONES[:, :], UB[:, i, :], first, last, [b_UB[i], b_ones], signal=True)
        mm(5, 0, T, ONES[:, :], USQ[:, i, :], first, last, [b_USQ[i], b_ones], signal=True)

    def rstd_from(bank_s2, n, eps, mean_slot, with_mean):
        inv = 1.0 / n
        if with_mean:
            P.op(dve, lambda: nc.vector.tensor_scalar(out=ST[:, 0, :], in0=PS[4][:, :], scalar1=inv, scalar2=None,
                                                      op0=ALU.mult), reads=[psb[4]], writes=[b_ST[0]])
            P.op(dve, lambda: nc.vector.tensor_tensor(out=ST[:, 3, :], in0=ST[:, 0, :], in1=ST[:, 0, :],
                                                      op=ALU.mult), reads=[b_ST[0]], writes=[b_ST[3]])
            P.op(dve, lambda: nc.vector.scalar_tensor_tensor(
                out=ST[:, 4, :], in0=PS[bank_s2][:, :], scalar=inv, in1=ST[:, 3, :], op0=ALU.mult,
                op1=ALU.subtract), reads=[psb[bank_s2], b_ST[3]], writes=[b_ST[4]])
            P.op(dve, lambda: nc.vector.tensor_scalar(out=ST[:, 4, :], in0=ST[:, 4, :], scalar1=0.0, scalar2=eps,
                                                      op0=ALU.max, op1=ALU.add), reads=[b_ST[4]], writes=[b_ST[4]])
        else:
            P.op(dve, lambda: nc.vector.tensor_scalar(out=ST[:, 4, :], in0=PS[bank_s2][:, :], scalar1=inv,
                                                      scalar2=eps, op0=ALU.mult, op1=ALU.add),
                 reads=[psb[bank_s2]], writes=[b_ST[4]])
        P.op(act, lambda: nc.scalar.activation(out=ST[:, 5, :], in_=ST[:, 4, :], func=AF.Sqrt),
             reads=[b_ST[4]], writes=[b_ST[5]])
        P.op(dve, lambda: nc.vector.reciprocal(out=ST[:, 1, :], in_=ST[:, 5, :]), reads=[b_ST[5]], writes=[b_ST[1]])
        if with_mean:
            P.op(dve, lambda: nc.vector.scalar_tensor_tensor(
                out=ST[:, 2, :], in0=ST[:, 0, :], scalar=-1.0, in1=ST[:, 1, :], op0=ALU.mult, op1=ALU.mult),
                reads=[b_ST[0], b_ST[1]], writes=[b_ST[2]])

    def ln_finish(l, j, s, last_sub, pool_next):
        rstd_from(5, D, LN_EPS / (ALPHA * ALPHA), 0, True)
        for m in range(NCH):
            P.op(dve, lambda: nc.vector.tensor_tensor(out=U[:, m, :], in0=U[:, m, :], in1=ST[:, 1, :], op=ALU.mult),
                 reads=[b_U[m], b_ST[1]], writes=[b_U[m]])
            P.op(pool, lambda: nc.gpsimd.tensor_tensor(out=U[:, m, :], in0=U[:, m, :], in1=ST[:, 2, :], op=ALU.add),
                 reads=[b_U[m], b_ST[2]], writes=[b_U[m]])
            P.op(act, lambda: nc.scalar.activation(out=X[:, m, :], in_=U[:, m, :], func=AF.Identity,
                                                   bias=LNB[:, l, j, m:m + 1], scale=LNG[:, l, j, m:m + 1]),
                 reads=[b_U[m], b_const], writes=[b_X[m]])
            if not last_sub:
                if pool_next:
                    P.op(act, lambda: nc.scalar.activation(
                        out=HW[:, m, 16:528], in_=U[:, m, :], func=AF.Identity,
                        bias=DRV[:, l, j, 3, m:m + 1, s], scale=DRV[:, l, j, 2, m:m + 1, s]),
                        reads=[b_U[m], b_MOD], writes=[b_HW[m]])
                else:
                    P.op(act, lambda: nc.scalar.activation(
                        out=H[:, m, :], in_=U[:, m, :], func=AF.Identity,
                        bias=DRV[:, l, j, 3, m:m + 1, s], scale=DRV[:, l, j, 2, m:m + 1, s]),
                        reads=[b_U[m], b_MOD], writes=[b_H[m]])

    def sublayer_outputs(l, j, s, xsrc, xsrc_b, produce):
        pend = []
        for m in range(NCH):
            bank = produce(m)
            epilogue_chunk(l, j, s, m, bank, xsrc, xsrc_b)
            pend.append(m)
            if len(pend) > 1:
                mmm = pend.pop(0)
                stats_chunk(mmm, mmm == 0, False)
        while pend:
            mmm = pend.pop(0)
            stats_chunk(mmm, mmm == 0, mmm == NCH - 1)

    def pool_mixer(l, s, first_tile, xsrc, xsrc_b):
        wslot = wload(wt_index(l, "pool"))
        Wp = WR[:, wslot, :]
        for m in range(NCH):
            eng = dve if m % 2 == 0 else pool
            P.op(eng, lambda: eng.h.tensor_copy(out=HW[:, m, 0:16], in_=HALO[:, l, m, :]),
                 reads=[b_HALO[l][m]], writes=[b_HW[m]])
        for m in range(NCH):
            g = m // 2
            w = 2 << g
            eng = dve if m % 2 == 0 else pool
            t0i = (m % 2) * 2
            src = HW[:, m, :]
            srcb = b_HW[m]
            sh = 1
            lo = 16 - (w - 1)
            k = 0
            while sh < w:
                lo2 = lo + sh
                dst = PTMP[:, t0i + (k % 2), :]
                dstb = b_PTMP[t0i + (k % 2)]
                P.op(eng, lambda: eng.h.tensor_tensor(out=dst[:, lo2:528], in0=src[:, lo2:528],
                                                      in1=src[:, lo2 - sh:528 - sh], op=ALU.add),
                     reads=[srcb], writes=[dstb])
                src, srcb = dst, dstb
                lo = lo2
                sh *= 2
                k += 1
            P.op(dve, lambda: nc.vector.scalar_tensor_tensor(out=H[:, m, :], in0=src[:, 16:528], scalar=1.0 / w,
                                                             in1=HW[:, m, 16:528], op0=ALU.mult, op1=ALU.subtract),
                 reads=[srcb, b_HW[m]], writes=[b_H[m]])
            if first_tile:
                P.op(eng, lambda: eng.h.tensor_tensor(out=src[:, 16:16 + w - 1], in0=src[:, 16:16 + w - 1],
                                                      in1=CST[:, 16:16 + w - 1], op=ALU.mult),
                     reads=[srcb, b_const], writes=[srcb])
                P.op(eng, lambda: eng.h.tensor_tensor(out=H[:, m, 0:w - 1], in0=src[:, 16:16 + w - 1],
                                                      in1=HW[:, m, 16:16 + w - 1], op=ALU.subtract),
                     reads=[srcb, b_HW[m]], writes=[b_H[m]])
            P.op(eng, lambda: eng.h.tensor_copy(out=HALO[:, l, m, :], in_=HW[:, m, 512:528]),
                 reads=[b_HW[m]], writes=[b_HALO[l][m]])

        def produce(m):
            g, mo = m // 2, m % 2
            bank = rot()
            for kc in range(2):
                off = g * 512 + kc * 256 + mo * 128
                mm(bank, 0, T, Wp[:, off:off + 128], H[:, 2 * g + kc, :], kc == 0, kc == 1,
                   [b_WR[wslot], b_H[2 * g + kc]])
            return bank

        sublayer_outputs(l, 0, s, xsrc, xsrc_b, produce)

    def mlp(l, s):
        for g in range(8):
            wslot = wload(wt_index(l, "w1", g))
            for jj in range(4):
                bank = rot()
                for kc in range(8):
                    mm(bank, 0, T, WR[:, wslot, kc * 512 + jj * 128: kc * 512 + (jj + 1) * 128], H[:, kc, :],
                       kc == 0, kc == 7, [b_WR[wslot], b_H[kc]])
                i = ubi[0] % 3
                ubi[0] += 1
                f = g * 4 + jj
                P.op(act, lambda: nc.scalar.activation(out=RT[:, i, :], in_=PS[bank][:, :], func=AF.Relu),
                     reads=[psb[bank]], writes=[b_RT[i]])
                P.op(pool, lambda: nc.gpsimd.tensor_tensor(out=A[:, f, :], in0=RT[:, i, :], in1=RT[:, i, :],
                                                           op=ALU.mult), reads=[b_RT[i]], writes=[b_A[f]])
        first_stat = [True]
        pend = []

        def flush_stats(n, final):
            k = 0
            while pend and k < n:
                mmm = pend.pop(0)
                stats_chunk(mmm, mmm == 0, mmm == NCH - 1)
                k += 1

        for half in range(2):
            banks = [0, 1, 2, 3] if half == 0 else [6, 7, 2, 3]
            if half == 1:
                banks = [6, 7, 0, 1]
            for jt in range(4):
                wslot = wload(wt_index(l, "w2", half * 4 + jt))
                for fl in range(8):
                    ffc = jt * 8 + fl
                    for q in range(4):
                        mm(banks[q], 0, T, WR[:, wslot, fl * 512 + q * 128: fl * 512 + (q + 1) * 128], A[:, ffc, :],
                           ffc == 0, ffc == 31, [b_WR[wslot], b_A[ffc]], signal=(ffc == 31 or (fl == 7 and q == 3)))
                if half == 1 and jt == 1:
                    flush_stats(4, False)
            for q in range(4):
                m = half * 4 + q
                epilogue_chunk(l, 1, s, m, banks[q], X, b_X)
                pend.append(m)
        flush_stats(8, True)

    def rope_tables(s, t0):
        src = bass.AP(pos_t, s * S + t0, [[0, 128], [1, T]])
        P.dma(sp, c_pos, POSI[:, :], src, writes=[b_POSI])
        P.op(dve, lambda: nc.vector.tensor_copy(out=TR[:, 0, :], in_=POSI[:, :]), reads=[b_POSI], writes=[b_TR[0]])
        P.op(dve, lambda: nc.vector.tensor_scalar(out=TR[:, 0, :], in0=TR[:, 0, :], scalar1=CST[:, 0:1], scalar2=None,
                                                  op0=ALU.mult), reads=[b_TR[0], b_const], writes=[b_TR[0]])
        a_ap, a_b = TR[:, 0, :], b_TR[0]
        P.op(dve, lambda: nc.vector.tensor_scalar(out=KI[:, :], in0=a_ap, scalar1=1.0 / TWO_PI, scalar2=None,
                                                  op0=ALU.mult), reads=[a_b], writes=[b_KI])
        P.op(dve, lambda: nc.vector.tensor_copy(out=TR[:, 2, :], in_=KI[:, :]), reads=[b_KI], writes=[b_TR[2]])
        P.op(dve, lambda: nc.vector.scalar_tensor_tensor(out=TR[:, 3, :], in0=TR[:, 2, :], scalar=-C1, in1=a_ap,
                                                         op0=ALU.mult, op1=ALU.add),
             reads=[b_TR[2], a_b], writes=[b_TR[3]])
        P.op(dve, lambda: nc.vector.scalar_tensor_tensor(out=TR[:, 3, :], in0=TR[:, 2, :], scalar=-C2,
                                                         in1=TR[:, 3, :], op0=ALU.mult, op1=ALU.add),
             reads=[b_TR[2], b_TR[3]], writes=[b_TR[3]])
        LIM = math.pi - 2e-6
        P.op(dve, lambda: nc.vector.tensor_scalar(out=TR[:, 3, :], in0=TR[:, 3, :], scalar1=LIM,
                                                  scalar2=-LIM, op0=ALU.min, op1=ALU.max),
             reads=[b_TR[3]], writes=[b_TR[3]])
        P.op(act, lambda: nc.scalar.activation(out=CS[:, 1, :], in_=TR[:, 3, :], func=AF.Sin,
                                               scale=CST[:, 1:2]), reads=[b_TR[3], b_const], writes=[b_CS])
        P.op(dve, lambda: nc.vector.tensor_scalar(out=TR[:, 1, :], in0=TR[:, 3, :], scalar1=math.pi / 2,
                                                  scalar2=None, op0=ALU.add), reads=[b_TR[3]], writes=[b_TR[1]])
        P.op(dve, lambda: nc.vector.tensor_scalar(out=TR[:, 2, :], in0=TR[:, 1, :], scalar1=math.pi,
                                                  scalar2=-TWO_PI, op0=ALU.is_gt, op1=ALU.mult),
             reads=[b_TR[1]], writes=[b_TR[2]])
        P.op(dve, lambda: nc.vector.tensor_tensor(out=TR[:, 1, :], in0=TR[:, 1, :], in1=TR[:, 2, :], op=ALU.add),
             reads=[b_TR[1], b_TR[2]], writes=[b_TR[1]])
        P.op(dve, lambda: nc.vector.tensor_scalar(out=TR[:, 1, :], in0=TR[:, 1, :], scalar1=LIM,
                                                  scalar2=-LIM, op0=ALU.min, op1=ALU.max),
             reads=[b_TR[1]], writes=[b_TR[1]])
        P.op(act, lambda: nc.scalar.activation(out=CS[:, 0, :], in_=TR[:, 1, :], func=AF.Sin),
             reads=[b_TR[1]], writes=[b_CS])

    def rope_apply(bankA, bankB, out_ap, out_bufs, qpair=None):
        P.op(dve, lambda: nc.vector.tensor_tensor(out=TR[:, 0, :], in0=PS[bankA][:, :], in1=CS[:, 0, :], op=ALU.mult),
             reads=[psb[bankA], b_CS], writes=[b_TR[0]])
        P.op(dve, lambda: nc.vector.tensor_tensor(out=TR[:, 1, :], in0=PS[bankB][:, :], in1=CS[:, 1, :], op=ALU.mult),
             reads=[psb[bankB], b_CS], writes=[b_TR[1]])
        if qpair is None:
            P.op(pool, lambda: nc.gpsimd.tensor_tensor(out=out_ap, in0=TR[:, 0, :], in1=TR[:, 1, :], op=ALU.add),
                 reads=[b_TR[0], b_TR[1]], writes=out_bufs)
        else:
            for hh in range(2):
                p0 = hh * 64
                P.op(pool, lambda: nc.gpsimd.tensor_tensor(out=QRz[p0:p0 + 64, 2 * qpair + hh, :],
                                                           in0=TR[p0:p0 + 64, 0, :], in1=TR[p0:p0 + 64, 1, :],
                                                           op=ALU.add),
                     reads=[b_TR[0], b_TR[1]], writes=[b_QR[2 * qpair + hh]])

    def rms_chunks(nch, srcs_bank_fn, g_ap_fn, out_ap_fn, out_bufs, eps, n):
        for mc in range(nch):
            bank = srcs_bank_fn(mc)
            P.op(act, lambda: nc.scalar.copy(out=C32[:, mc, :], in_=PS[bank][:, :]),
                 reads=[psb[bank]], writes=[b_C32[mc]])
            i = ubi[0] % 3
            ubi[0] += 1
            P.op(act, lambda: nc.scalar.activation(out=USQ[:, i, :], in_=C32[:, mc, :], func=AF.Square),
                 reads=[b_C32[mc]], writes=[b_USQ[i]])
            mm(5, 0, T, ONES[:, :], USQ[:, i, :], mc == 0, mc == nch - 1, [b_USQ[i], b_ones], signal=True)
        rstd_from(5, n, eps, 0, False)
        for mc in range(nch):
            P.op(dve, lambda: nc.vector.scalar_tensor_tensor(
                out=out_ap_fn(mc), in0=C32[:, mc, :], scalar=g_ap_fn(mc), in1=ST[:, 1, :], op0=ALU.mult,
                op1=ALU.mult), reads=[b_C32[mc], b_ST[1], b_const], writes=[out_bufs[mc]])

    def kv_phase(s, ti):
        t0 = ti * T
        for m in range(NCH):
            P.op(pool, lambda: nc.gpsimd.tensor_copy(out=A[:, 8 + m, :], in_=X[:, m, :]),
                 reads=[b_X[m]], writes=[b_A[8 + m]])
        ws = wload(wt_index(0, "kvin"))

        def proj(mc):
            bank = rot()
            for kc in range(8):
                mm(bank, 0, T, WR[:, ws, kc * 512 + mc * 128: kc * 512 + (mc + 1) * 128], A[:, 8 + kc, :],
                   kc == 0, kc == 7, [b_WR[ws], b_A[8 + kc]])
            return bank

        rms_chunks(2, proj, lambda mc: KVG[:, mc:mc + 1], lambda mc: CKVN[:, mc, :], b_CKVN, RMS_EPS, KVR)
        bA = proj(2)
        bB = proj(3)
        rope_apply(bA, bB, KRS[:, t0:t0 + T], [b_KRS[ti]])
        ws2 = wload(wt_index(0, "kvup"))
        for h in range(NH):
            bank = rot()
            for kc in range(2):
                mm(bank, 0, T, WR[:, ws2, kc * 1024 + h * 128: kc * 1024 + (h + 1) * 128], CKVN[:, kc, :],
                   kc == 0, kc == 1, [b_WR[ws2], b_CKVN[kc]])
            eng = act if h % 2 == 0 else dve
            if h % 2 == 0:
                P.op(act, lambda: nc.scalar.copy(out=A[:, 16 + h, :], in_=PS[bank][:, :]),
                     reads=[psb[bank]], writes=[b_A[16 + h]])
            else:
                P.op(dve, lambda: nc.vector.tensor_copy(out=A[:, 16 + h, :], in_=PS[bank][:, :]),
                     reads=[psb[bank]], writes=[b_A[16 + h]])
        P.dma(pool, c_kst, kT_d[s, :, :, t0:t0 + T].rearrange("h p t -> p h t"), A[:, 16:24, :],
              reads=b_A[16:24], writes=[b_kd[s][ti]])
        for ks in range(4):
            for hg in range(2):
                bank = rot()
                for kc in range(2):
                    mm(bank, 0, T, CKVN[:, kc, ks * 128:(ks + 1) * 128],
                       WR[:, ws2, 2048 + kc * 1024 + hg * 512: 2048 + kc * 1024 + (hg + 1) * 512],
                       kc == 0, kc == 1, [b_WR[ws2], b_CKVN[kc]])
                o = A[:, 24 + hg * 4: 24 + hg * 4 + 4, ks * 128:(ks + 1) * 128]
                i_ = PS[bank][:, :].rearrange("p (h d) -> p h d", d=128)
                if (ks + hg) % 2 == 0:
                    P.op(act, lambda: nc.scalar.copy(out=o, in_=i_), reads=[psb[bank]],
                         writes=b_A[24 + hg * 4: 24 + hg * 4 + 4])
                else:
                    P.op(dve, lambda: nc.vector.tensor_copy(out=o, in_=i_), reads=[psb[bank]],
                         writes=b_A[24 + hg * 4: 24 + hg * 4 + 4])
        P.dma(pool, c_vst, v_d[s, :, :, ti * 4:(ti + 1) * 4, :].rearrange("h p k d -> p h (k d)"), A[:, 24:32, :],
              reads=b_A[24:32], writes=[b_vd[s][ti]])

    kb_state = [0]

    def attention(l, s, ti):
        jl = l - NA
        ws = wload(wt_index(l, "qd"))

        def projq(mc):
            bank = rot()
            for kc in range(8):
                mm(bank, 0, T, WR[:, ws, kc * 512 + mc * 128: kc * 512 + (mc + 1) * 128], H[:, kc, :],
                   kc == 0, kc == 7, [b_WR[ws], b_H[kc]])
            return bank

        rms_chunks(4, projq, lambda mc: QNG[:, jl, mc:mc + 1], lambda mc: CQN[:, mc, :], b_CQN, RMS_EPS, QR)
        wa = wload(wt_index(l, "quA"))
        for h in range(NH):
            bank = rot()
            for kc in range(4):
                mm(bank, 0, T, WR[:, wa, kc * 1024 + h * 128: kc * 1024 + (h + 1) * 128], CQN[:, kc, :],
                   kc == 0, kc == 3, [b_WR[wa], b_CQN[kc]])
            if h % 2 == 0:
                P.op(act, lambda: nc.scalar.copy(out=A[:, h, :], in_=PS[bank][:, :]),
                     reads=[psb[bank]], writes=[b_A[h]])
            else:
                P.op(dve, lambda: nc.vector.tensor_copy(out=A[:, h, :], in_=PS[bank][:, :]),
                     reads=[psb[bank]], writes=[b_A[h]])
        wb = wload(wt_index(l, "quB"))
        for pp in range(4):
            bA = rot()
            for kc in range(4):
                mm(bA, 0, T, WR[:, wb, kc * 1024 + pp * 128: kc * 1024 + (pp + 1) * 128], CQN[:, kc, :],
                   kc == 0, kc == 3, [b_WR[wb], b_CQN[kc]])
            bB = rot()
            for kc in range(4):
                mm(bB, 0, T, WR[:, wb, kc * 1024 + 512 + pp * 128: kc * 1024 + 512 + (pp + 1) * 128], CQN[:, kc, :],
                   kc == 0, kc == 3, [b_WR[wb], b_CQN[kc]])
            rope_apply(bA, bB, None, None, qpair=pp)
        pti = [0]
        for h in range(NH):
            ob = 4 + (h % 2)
            lb = 6 + (h % 2)
            pr = (h % 2) * 64
            n_units = (ti + 1) * 4
            u_i = 0
            for kb in range(ti + 1):
                slot = kb_state[0] % KB_N
                kb_state[0] += 1
                P.dma(sp, c_kb[slot], KBLK[:, slot, :], kT_d[s, h, :, kb * T:(kb + 1) * T],
                      reads=[b_kd[s][kb]], writes=[b_KBLK[slot]])
                P.dma(sp, c_vb[slot], VBLK[:, slot, :, :], v_d[s, h, :, kb * 4:(kb + 1) * 4, :],
                      reads=[b_vd[s][kb]], writes=[b_VBLK[slot]])
                diag = kb == ti
                for ks in range(4):
                    c0 = ks * 128 if diag else 0
                    sbank = rot()
                    mm(sbank, c0, T, KBLK[:, slot, ks * 128:(ks + 1) * 128], A[:, h, c0:T], True, False,
                       [b_KBLK[slot], b_A[h]], signal=False)
                    kcol = kb * T + ks * 128
                    mm(sbank, c0, T, KRS[:, kcol:kcol + 128], QRz[:, h, c0:T], False, True,
                       [b_KRS[kb], b_QR[h]])
                    pi = pti[0] % 3
                    pti[0] += 1
                    P.op(act, lambda: nc.scalar.activation(out=PT[:, pi, c0:T], in_=PS[sbank][:, c0:T], func=AF.Exp,
                                                           scale=ATTN_SCALE),
                         reads=[psb[sbank]], writes=[b_PT[pi]])
                    if diag:
                        P.op(pool, lambda: nc.gpsimd.tensor_tensor(out=PT[:, pi, c0:c0 + 128],
                                                                   in0=PT[:, pi, c0:c0 + 128], in1=TRI[:, :],
                                                                   op=ALU.mult),
                             reads=[b_PT[pi], b_const], writes=[b_PT[pi]])
                    first = u_i == 0
                    last = u_i == n_units - 1
                    mm(ob, c0, T, VBLK[:, slot, ks, :], PT[:, pi, c0:T], first, last, [b_VBLK[slot], b_PT[pi]])
                    mm(lb, c0, T, ONES[:, :], PT[:, pi, c0:T], first, last, [b_PT[pi], b_ones], signal=True)
                    u_i += 1
            si = 3 + (h % 2)
            P.op(dve, lambda: nc.vector.reciprocal(out=ST[:, si, :], in_=PS[lb][:, :]),
                 reads=[psb[lb]], writes=[b_ST[si]])
            P.op(dve, lambda: nc.vector.tensor_tensor(out=A[:, 8 + h, :], in0=PS[ob][:, :], in1=ST[:, si, :],
                                                      op=ALU.mult),
                 reads=[psb[ob], b_ST[si]], writes=[b_A[8 + h]])
        wos = [None, None]

        def produce(m):
            half = m // 4
            if wos[half] is None:
                wos[half] = wload(wt_index(l, "wo", half))
            wsl = wos[half]
            bank = rot()
            for h in range(NH):
                mm(bank, 0, T, WR[:, wsl, h * 512 + (m % 4) * 128: h * 512 + (m % 4 + 1) * 128], A[:, 8 + h, :],
                   h == 0, h == NH - 1, [b_WR[wsl], b_A[8 + h]])
            return bank

        sublayer_outputs(l, 0, s, X, b_X, produce)

    order = [(s, ti) for s in range(NSEQ) for ti in range(NTILE)]

    def load_x(s, ti):
        P.dma(sp, c_xin, xin[:, :, :], xT[s, :, ti * T:(ti + 1) * T].rearrange("(c p) t -> p c t", p=128),
              writes=b_xin)

    load_x(*order[0])
    for oi, (s, ti) in enumerate(order):
        if ti == 0 and s > 0:
            P.op(dve, lambda: nc.vector.memset(HALO[:, :, :, :], 0.0), writes=[b for l in b_HALO for b in l])
        rope_tables(s, ti * T)
        if oi == 0:
            checkpoint("rope", CS[:, :, :].rearrange("p a t -> p (a t)"), [b_CS])
        for m in range(NCH):
            P.op(act, lambda: nc.scalar.activation(out=HW[:, m, 16:528], in_=xin[:, m, :], func=AF.Identity,
                                                   bias=H0C[:, 1, m:m + 1, s], scale=H0C[:, 0, m:m + 1, s]),
                 reads=[b_xin[m], b_MOD], writes=[b_HW[m]])
        for l in range(DEPTH):
            xsrc, xsrc_b = (xin, b_xin) if l == 0 else (X, b_X)
            if l < NA:
                pool_mixer(l, s, ti == 0, xsrc, xsrc_b)
            else:
                attention(l, s, ti)
            ln_finish(l, 0, s, False, False)
            if oi == 0:
                checkpoint(f"x{l}0", X[:, :, :].rearrange("p c t -> p (c t)"), b_X)
            if l == 0 and oi + 1 < len(order):
                load_x(*order[oi + 1])
            mlp(l, s)
            last = l == DEPTH - 1
            ln_finish(l, 1, s, last, (l + 1) < NA)
            if oi == 0:
                checkpoint(f"x{l}1", X[:, :, :].rearrange("p c t -> p (c t)"), b_X)
            if l == NA - 1:
                kv_phase(s, ti)
                if oi == 0:
                    checkpoint("kv")
        P.dma(pool, c_out, outT[s, :, ti * T:(ti + 1) * T].rearrange("(c p) t -> p c t", p=128), X[:, :, :],
              reads=b_X, writes=[b_out])

    assert P.check_deadlock(), "deadlock in generated program"
    P.wait_all(pool, [c_out, c_kst, c_vst, c_cvt])
    P.wait_all(sp, c_kb + c_vb + c_wr + [c_xin, c_pos, c_const, c_dbg] + c_ada)
    es.close()
    return nc


def make_core_inputs(inp, seqs, S):
    x = inp["x"][seqs, :S]
    m = {}
    m["xT"] = np.ascontiguousarray(np.transpose(x, (0, 2, 1)))
    m["pos"] = np.ascontiguousarray(inp["positions"][seqs, :S]).astype(np.int32)
    m["cT"] = np.ascontiguousarray(np.transpose(fm(inp["c"][seqs]), (0, 2, 1)))
    return m


def make_shared_inputs(inp):
    m = {}
    m["adaw"] = build_ada_image(np.asarray(inp["ada_w"], np.float32))
    m["adab"] = fm(inp["ada_b"])
    m["lng"] = fm(inp["ln_g"])
    m["lnb"] = fm(inp["ln_b"])
    m["psc"] = fm(inp["pool_scale"])
    m["qng"] = fm(inp["q_norm_g"])
    m["kvg"] = fm(inp["kv_norm_g"])
    cst = np.zeros((128, 32), np.float32)
    inv_freq = (10000.0 ** (-np.arange(0, 64, 2, dtype=np.float32) / np.float32(64))).astype(np.float32)
    p = np.arange(128)
    cst[:, 0] = inv_freq[p % 32]
    cst[:, 1] = np.where((p % 64) < 32, -1.0, 1.0)
    cst[:, 16:32] = 1.0 / (np.arange(16, dtype=np.float32) + 1.0)
    m["cst"] = cst
    k = np.arange(128)[:, None]
    q = np.arange(128)[None, :]
    m["tri"] = (k <= q).astype(np.float32).astype(ml_dtypes.bfloat16)
    m["wimg32"] = build_weight_image(inp)
    return m


_NC_CACHE = {}


def run(inp, n_cores, NSEQ, S, trace=False, stop=None):
    inp = {k: np.asarray(v) for k, v in inp.items()}
    shared = make_shared_inputs(inp)
    in_maps = []
    for c in range(n_cores):
        seqs = list(range(c * NSEQ, (c + 1) * NSEQ))
        m = dict(shared)
        m.update(make_core_inputs(inp, seqs, S))
        in_maps.append(m)
    key = (NSEQ, S)
    nc = build_program(NSEQ, S, stop)
    res = run_bass_kernel_spmd(nc, in_maps, core_ids=list(range(n_cores)), trace=trace)
    outs = [np.transpose(r["outT"], (0, 2, 1)) for r in res.results]
    return np.ascontiguousarray(np.concatenate(outs, axis=0)).astype(np.float32), res


def kernel(**inputs):
    out, _ = run(inputs, 8, 2, 4096)
    return out
```

```python
import math
from contextlib import ExitStack

import numpy as np
import ml_dtypes
import concourse.bass as bass
import concourse.mybir as mybir
from concourse.bass_utils import run_bass_kernel_spmd

F32 = mybir.dt.float32
BF16 = mybir.dt.bfloat16
I32 = mybir.dt.int32
ALU = mybir.AluOpType
AF = mybir.ActivationFunctionType

D = 1024
NCH = 8
DEPTH = 4
NA = 2
DFF = 4096
NH = 8
QR = 512
KVR = 256
T = 512
ALPHA = (2.0 * DEPTH) ** 0.25
LN_EPS = 1e-5
RMS_EPS = 1e-6
ATTN_SCALE = (128 + 64) ** -0.5
N_WT_POOL = 17
N_WT_ATT = 21
TWO_PI = 2.0 * math.pi
C1 = 6.28125
C2 = TWO_PI - C1


class Ctr:
    def __init__(self, sem, name):
        self.sem = sem
        self.name = name
        self.count = 0


class Buf:
    __slots__ = ("name", "lw", "rd")

    def __init__(self, name):
        self.name = name
        self.lw = None
        self.rd = {}


class Eng:
    def __init__(self, name, h, ctr, is_pe=False):
        self.name = name
        self.h = h
        self.ctr = ctr
        self.waited = {}
        self.is_pe = is_pe


def _flat(lst):
    out = []
    for b in lst:
        if isinstance(b, (list, tuple)):
            out.extend(_flat(b))
        else:
            out.append(b)
    return out


class Prog:
    def __init__(self, nc, es):
        self.nc = nc
        self.es = es
        self.nsem = 0
        self.stopped = False
        self.log = {}
        self.pend = {}

        def mk(name, h, is_pe=False):
            return Eng(name, h, self.new_ctr(name), is_pe)

        self.pe = mk("pe", nc.tensor, True)
        self.act = mk("act", nc.scalar)
        self.dve = mk("dve", nc.vector)
        self.pool = mk("pool", nc.gpsimd)
        self.sp = mk("sp", nc.sync)

    def new_ctr(self, name):
        sem = self.es.enter_context(self.nc.semaphore("s_" + name))
        self.nsem += 1
        return Ctr(sem, name)

    def _deps(self, eng, reads, writes, own_ctr):
        deps = {}

        def add(c, v):
            if deps.get(c, 0) < v:
                deps[c] = v

        for b in reads:
            if b.lw is not None:
                add(*b.lw)
        for b in writes:
            if b.lw is not None:
                add(*b.lw)
            for c, v in b.rd.items():
                add(c, v)
        for c, v in deps.items():
            if c is own_ctr and eng.is_pe:
                continue
            if eng.waited.get(c, 0) >= v:
                continue
            if c is own_ctr and v > c.count:
                continue
            eng.h.wait_ge(c.sem, v)
            eng.waited[c] = v

    def op(self, eng, fn, reads=(), writes=(), signal=True):
        if self.stopped:
            return None
        c = eng.ctr
        reads = _flat(reads)
        writes = _flat(writes)
        self._deps_compute(eng, reads, writes)
        ins = fn()
        self.log.setdefault(eng.name, []).append((self.pend.pop(eng.name, []), c if signal else None, 1, "op"))
        if signal:
            c.count += 1
            ins.then_inc(c.sem, 1)
            v = c.count
        else:
            v = c.count + 1
        for b in writes:
            b.lw = (c, v)
            b.rd = {}
        for b in reads:
            if b.rd.get(c, 0) < v:
                b.rd[c] = v
        return ins

    def _deps_compute(self, eng, reads, writes):
        own = eng.ctr
        deps = {}

        def add(c, v):
            if deps.get(c, 0) < v:
                deps[c] = v

        for b in reads:
            if b.lw is not None:
                add(*b.lw)
        for b in writes:
            if b.lw is not None and b.lw[0] is not own:
                add(*b.lw)
            for c, v in b.rd.items():
                if c is not own:
                    add(c, v)
        for c, v in deps.items():
            if c is own:
                if eng.is_pe:
                    continue
                if v > c.count:
                    continue
            if eng.waited.get(c, 0) >= v:
                continue
            eng.h.wait_ge(c.sem, v)
            eng.waited[c] = v
            self.pend.setdefault(eng.name, []).append((c, v))

    def dma(self, eng, ctr, out, in_, reads=(), writes=(), **kw):
        if self.stopped:
            return None
        reads = _flat(reads)
        writes = _flat(writes)
        deps = {}

        def add(c, v):
            if deps.get(c, 0) < v:
                deps[c] = v

        for b in reads:
            if b.lw is not None:
                add(*b.lw)
        for b in writes:
            if b.lw is not None:
                add(*b.lw)
            for c, v in b.rd.items():
                add(c, v)
        for c, v in deps.items():
            if eng.waited.get(c, 0) >= v:
                continue
            eng.h.wait_ge(c.sem, v)
            eng.waited[c] = v
            self.pend.setdefault(eng.name, []).append((c, v))
        ins = eng.h.dma_start(out=out, in_=in_, **kw)
        self.log.setdefault(eng.name, []).append((self.pend.pop(eng.name, []), ctr, 16, "dma"))
        ctr.count += 16
        ins.then_inc(ctr.sem, 16)
        v = ctr.count
        for b in writes:
            b.lw = (ctr, v)
            b.rd = {}
        for b in reads:
            if b.rd.get(ctr, 0) < v:
                b.rd[ctr] = v
        return ins

    def check_deadlock(self):
        val = {}
        pos = {k: 0 for k in self.log}
        progress = True
        while progress:
            progress = False
            for k, lst in self.log.items():
                while pos[k] < len(lst):
                    waits, c, inc, kind = lst[pos[k]]
                    if all(val.get(cc, 0) >= vv for cc, vv in waits):
                        if c is not None:
                            val[c] = val.get(c, 0) + inc
                        pos[k] += 1
                        progress = True
                    else:
                        break
        stuck = {k: pos[k] for k in self.log if pos[k] < len(self.log[k])}
        for k, p in stuck.items():
            waits, c, inc, kind = self.log[k][p]
            print("DEADLOCK: engine", k, "instr", p, "/", len(self.log[k]), kind, "waits",
                  [(cc.name, vv, val.get(cc, 0)) for cc, vv in waits])
        return not stuck

    def wait_all(self, eng, ctrs):
        for c in ctrs:
            if c.count > 0 and eng.waited.get(c, 0) < c.count:
                eng.h.wait_ge(c.sem, c.count)
                eng.waited[c] = c.count


def _wtile_from_rows(w, kcs, cols):
    K = w.shape[0]
    assert K == kcs * 128
    sub = w[:, cols]
    return np.ascontiguousarray(sub.reshape(kcs, 128, len(cols)).transpose(1, 0, 2).reshape(128, -1))


def build_weight_image(inp):
    tiles = []

    def pad(t):
        out = np.zeros((128, 4096), np.float32)
        out[:, : t.shape[1]] = t
        return out

    def mlp_tiles(l):
        w1 = inp["mlp_w1"][l]
        w2 = inp["mlp_w2"][l]
        for g in range(8):
            tiles.append(pad(_wtile_from_rows(w1, 8, np.arange(g * 512, (g + 1) * 512))))
        for half in range(2):
            for jt in range(4):
                rows = w2[jt * 1024:(jt + 1) * 1024, half * 512:(half + 1) * 512]
                tiles.append(pad(_wtile_from_rows(rows, 8, np.arange(512))))

    rope_perm = np.concatenate([np.arange(32, 64), np.arange(0, 32)])
    for l in range(DEPTH):
        if l < NA:
            pw = inp["pool_w"][l]
            t = np.concatenate([_wtile_from_rows(pw[g], 2, np.arange(256)) for g in range(4)], axis=1)
            tiles.append(pad(t))
        else:
            j = l - NA
            tiles.append(pad(_wtile_from_rows(inp["q_down_w"][j], 8, np.arange(512))))
            qu = inp["q_up_w"][j]
            nope_cols = np.concatenate([np.arange(h * 192, h * 192 + 128) for h in range(NH)])
            tiles.append(pad(_wtile_from_rows(qu, 4, nope_cols)))
            rope_cols = np.concatenate([np.arange(h * 192 + 128, h * 192 + 192) for h in range(NH)])
            rot_cols = np.concatenate([h * 192 + 128 + rope_perm for h in range(NH)])
            tiles.append(pad(_wtile_from_rows(qu, 4, np.concatenate([rope_cols, rot_cols]))))
            wo = inp["attn_out_w"][j]
            for half in range(2):
                tiles.append(pad(_wtile_from_rows(wo, 8, np.arange(half * 512, (half + 1) * 512))))
        mlp_tiles(l)
    kvw = inp["kv_in_w"]
    kr = np.arange(256, 320)
    kcols = np.concatenate([np.arange(256), kr, kr, 256 + rope_perm, 256 + rope_perm])
    tiles.append(pad(_wtile_from_rows(kvw, 8, kcols)))
    t = np.concatenate([_wtile_from_rows(inp["k_up_w"], 2, np.arange(1024)),
                        _wtile_from_rows(inp["v_up_w"], 2, np.arange(1024))], axis=1)
    tiles.append(pad(t))
    return np.stack(tiles, 0)


def wt_index(l, which, i=0):
    base = 0
    for ll in range(l):
        base += N_WT_POOL if ll < NA else N_WT_ATT
    if which == "kvin":
        return NA * N_WT_POOL + (DEPTH - NA) * N_WT_ATT
    if which == "kvup":
        return NA * N_WT_POOL + (DEPTH - NA) * N_WT_ATT + 1
    if l < NA:
        off = {"pool": 0, "w1": 1, "w2": 9}[which]
    else:
        off = {"qd": 0, "quA": 1, "quB": 2, "wo": 3, "w1": 5, "w2": 13}[which]
    return base + off + i


NT_W = NA * N_WT_POOL + (DEPTH - NA) * N_WT_ATT + 2


def fm(v):
    v = np.asarray(v, np.float32)
    lead = v.shape[:-1]
    n = v.shape[-1] // 128
    r = v.reshape(*lead, n, 128)
    return np.ascontiguousarray(np.moveaxis(r, -1, 0))


def build_ada_image(ada_w):
    a = ada_w.reshape(DEPTH, 8, 128, 12, 512)
    return np.ascontiguousarray(a.transpose(0, 3, 2, 1, 4))


class _Stop(Exception):
    pass


def build_program(NSEQ, S, stop=None):
    NTILE = S // T
    NKB = S // T
    nc = bass.Bass("TRN2", target_bir_lowering=False)
    es = ExitStack()

    def din(name, shape, dt=F32):
        return nc.dram_tensor(name, list(shape), dt, kind="ExternalInput").ap()

    xT = din("xT", [NSEQ, D, S])
    pos_t = nc.dram_tensor("pos", [NSEQ, S], I32, kind="ExternalInput")
    cT_d = din("cT", [128, NCH, NSEQ])
    adaw_d = din("adaw", [DEPTH, 12, 128, 8, 512])
    adab_d = din("adab", [128, DEPTH, 48])
    lng_d = din("lng", [128, DEPTH, 2, NCH])
    lnb_d = din("lnb", [128, DEPTH, 2, NCH])
    psc_d = din("psc", [128, NA, NCH])
    qng_d = din("qng", [128, DEPTH - NA, 4])
    kvg_d = din("kvg", [128, 2])
    cst_d = din("cst", [128, 32])
    tri_d = din("tri", [128, 128], BF16)
    wimg32 = din("wimg32", [NT_W, 128, 4096])
    outT = nc.dram_tensor("outT", [NSEQ, D, S], F32, kind="ExternalOutput").ap()
    dbg_d = nc.dram_tensor("dbg", [128, 4096], F32, kind="ExternalOutput").ap() if stop else None
    wimg = nc.dram_tensor("wimg", [NT_W, 128, 4096], BF16, kind="Internal").ap()
    kT_d = nc.dram_tensor("kT_d", [NSEQ, NH, 128, S], BF16, kind="Internal").ap()
    v_d = nc.dram_tensor("v_d", [NSEQ, NH, 128, NKB * 4, 128], BF16, kind="Internal").ap()

    P = Prog(nc, es)
    pe, act, dve, pool, sp = P.pe, P.act, P.dve, P.pool, P.sp

    def sb(name, shape, dt):
        return es.enter_context(nc.sbuf_tensor(name, list(shape), dt))

    xin = sb("xin", [128, NCH, T], F32)
    X = sb("X", [128, NCH, T], F32)
    U = sb("U", [128, NCH, T], F32)
    H = sb("H", [128, NCH, T], BF16)
    A = sb("A", [128, 32, T], BF16)
    WR_N = 4
    WR = sb("WR", [128, WR_N, 4096], BF16)
    UB = sb("UB", [128, 3, T], BF16)
    USQ = sb("USQ", [128, 3, T], BF16)
    RT = sb("RT", [128, 3, T], F32)
    ST = sb("ST", [128, 6, T], F32)
    TR = sb("TR", [128, 4, T], F32)
    CS = sb("CS", [128, 2, T], F32)
    POSI = sb("POSI", [128, T], I32)
    KI = POSI
    QRz = sb("QRz", [128, NH, T], BF16)
    CQN = sb("CQN", [128, 4, T], BF16)
    CKVN = sb("CKVN", [128, 2, T], BF16)
    KRS = sb("KRS", [128, S], BF16)
    KB_N = 6
    KBLK = sb("KBLK", [128, KB_N, T], BF16)
    VBLK = sb("VBLK", [128, KB_N, 4, 128], BF16)
    PT_N = 4
    PT = sb("PT", [128, PT_N, T], BF16)
    HALO = sb("HALO", [128, NA, NCH, 16], F32)
    MOD = sb("MOD", [128, DEPTH, 48, NSEQ], F32)
    DRV = sb("DRV", [128, DEPTH, 2, 5, NCH, NSEQ], F32)
    H0C = sb("H0C", [128, 2, NCH, NSEQ], F32)
    LNG = sb("LNG", [128, DEPTH, 2, NCH], F32)
    LNB = sb("LNB", [128, DEPTH, 2, NCH], F32)
    PSC = sb("PSC", [128, NA, NCH], F32)
    QNG = sb("QNG", [128, DEPTH - NA, 4], F32)
    KVG = sb("KVG", [128, 2], F32)
    CST = sb("CST", [128, 32], F32)
    ADAB = sb("ADAB", [128, DEPTH, 48], F32)
    CTs = sb("CTs", [128, NCH, NSEQ], F32)
    SC = sb("SC", [128, NCH, NSEQ], F32)
    TRI = sb("TRI", [128, 128], BF16)
    ONES = sb("ONES", [128, 128], BF16)

    Aflat = A[:, :, :]
    QT = A
    A32 = A[:, :, :].rearrange("p c t -> p (c t)").bitcast(F32)
    ADAT = A32.rearrange("p (s k c) -> p s k c", s=2, k=8)
    HW = A32[:, 0:NCH * 528].rearrange("p (m t) -> p m t", t=528)
    PTMP = A32[:, 17 * 256:17 * 256 + 4 * 528].rearrange("p (k t) -> p k t", t=528)
    C32 = A32[:, 0:4 * T].rearrange("p (m t) -> p m t", t=T)

    PS = [es.enter_context(nc.psum_tensor(f"ps{i}", [128, T], F32)) for i in range(8)]
    psb = [Buf(f"ps{i}") for i in range(8)]
    rot_state = [0]

    def rot():
        i = rot_state[0] % 4
        rot_state[0] += 1
        return i

    def bl(name, n):
        return [Buf(f"{name}{i}") for i in range(n)]

    b_xin, b_X, b_U, b_H = bl("xin", NCH), bl("X", NCH), bl("U", NCH), bl("H", NCH)
    b_A = bl("A", 32)
    b_WR = bl("WR", WR_N)
    b_UB, b_USQ, b_RT = bl("UB", 3), bl("USQ", 3), bl("RT", 3)
    b_ST, b_TR = bl("ST", 6), bl("TR", 4)
    b_CS = Buf("CS")
    b_POSI = Buf("POSI")
    b_KI = b_POSI
    b_QR, b_CQN, b_CKVN = bl("QR", NH), bl("CQN", 4), bl("CKVN", 2)
    b_ones = Buf("ones")
    b_KRS = bl("KRS", NKB)
    b_KBLK, b_VBLK = bl("KBLK", KB_N), bl("VBLK", KB_N)
    b_PT = bl("PT", PT_N)
    b_HALO = [bl(f"HALO{l}_", NCH) for l in range(NA)]
    def a_cover(b0, b1):
        return b_A[b0 // 1024:(b1 - 1) // 1024 + 1]

    b_PTMP = [a_cover(17408 + k * 2112, 17408 + (k + 1) * 2112) for k in range(4)]
    b_HW = [a_cover(m * 2112, (m + 1) * 2112) for m in range(NCH)]
    b_C32 = [b_A[2 * m:2 * m + 2] for m in range(4)]
    b_ADAT = [b_A[0:16], b_A[16:32]]
    b_const = Buf("const")
    b_MOD = Buf("MOD")
    b_wimg = Buf("wimg")
    b_kd = [bl(f"kd{s}_", NKB) for s in range(NSEQ)]
    b_vd = [bl(f"vd{s}_", NKB) for s in range(NSEQ)]
    b_out = Buf("out")
    b_TMPS = Buf("TMPS")

    c_const = P.new_ctr("const")
    c_cvt = P.new_ctr("cvt")
    c_ada = [P.new_ctr(f"ada{i}") for i in range(2)]
    c_wr = [P.new_ctr(f"wr{i}") for i in range(WR_N)]
    c_xin = P.new_ctr("xin")
    c_out = P.new_ctr("out")
    c_kst = P.new_ctr("kst")
    c_vst = P.new_ctr("vst")
    c_kb = [P.new_ctr(f"kb{i}") for i in range(KB_N)]
    c_vb = [P.new_ctr(f"vb{i}") for i in range(KB_N)]
    c_pos = P.new_ctr("pos")

    c_dbg = P.new_ctr("dbg")

    def checkpoint(label, ap=None, bufs=()):
        if stop != label:
            return
        if ap is not None:
            n = ap.shape[1]
            P.dma(sp, c_dbg, dbg_d[:, 0:n], ap, reads=list(bufs))
        P.stopped = True

    w32v = wimg32.rearrange("n p (a c) -> (n p a) c", c=2048)
    wbv = wimg.rearrange("n p (a c) -> (n p a) c", c=2048)
    rows_total = NT_W * 128 * 2
    RCH = 1024
    for r0 in range(0, rows_total, RCH):
        r1 = min(rows_total, r0 + RCH)
        P.dma(pool, c_cvt, wbv[r0:r1, :], w32v[r0:r1, :], writes=[b_wimg])
    b_wimg.lw = (c_cvt, c_cvt.count)

    for dst, src in ((CTs, cT_d), (ADAB, adab_d), (LNG, lng_d), (LNB, lnb_d), (PSC, psc_d),
                     (QNG, qng_d), (KVG, kvg_d), (CST, cst_d), (TRI, tri_d)):
        nd = len(dst.shape)
        sl = tuple([slice(None)] * nd)
        P.dma(sp, c_const, dst[sl], src[sl], writes=[b_const])
    b_const.lw = (c_const, c_const.count)
    P.op(dve, lambda: nc.vector.memset(ONES[:, :], 1.0), writes=[b_ones])
    P.op(pool, lambda: nc.gpsimd.memset(QRz[:, :, :], 0.0), writes=b_QR)
    P.op(dve, lambda: nc.vector.memset(HALO[:, :, :, :], 0.0), writes=[b for l in b_HALO for b in l])
    P.op(act, lambda: nc.scalar.activation(out=SC[:, :, :], in_=CTs[:, :, :], func=AF.Silu),
         reads=[b_const], writes=[b_MOD])

    for l in range(DEPTH):
        bank = 4 + (l % 2)
        for g in range(12):
            slot = (l * 12 + g) % 2
            P.dma(sp, c_ada[slot], ADAT[:, slot, :, :], adaw_d[l, g, :, :, :], writes=[b_ADAT[slot]])
            for j in range(4):
                m = g * 4 + j
                for kc in range(8):
                    P.op(pe, lambda: nc.tensor.matmul(
                        PS[bank][:, m * NSEQ:(m + 1) * NSEQ], lhsT=ADAT[:, slot, kc, j * 128:(j + 1) * 128],
                        rhs=SC[:, kc, :], start=(kc == 0), stop=(kc == 7)),
                        reads=[b_ADAT[slot], b_MOD], writes=[psb[bank]], signal=(kc == 7))
        for s in range(NSEQ):
            pv = PS[bank][:, 0:48 * NSEQ].rearrange("p (m s) -> p m s", s=NSEQ)
            P.op(dve, lambda: nc.vector.tensor_tensor(out=MOD[:, l, :, s], in0=pv[:, :, s], in1=ADAB[:, l, :],
                                                      op=ALU.add),
                 reads=[psb[bank], b_const], writes=[b_MOD])

    checkpoint("cvt")
    checkpoint("mod", MOD[:, :, :, :].rearrange("p l m s -> p (l m s)"), [b_MOD])
    def modv(l, j, k, s):
        o = (j * 3 + k) * NCH
        return MOD[:, l, o:o + NCH, s]

    for s in range(NSEQ):
        P.op(dve, lambda: nc.vector.tensor_scalar(out=H0C[:, 0, :, s], in0=modv(0, 0, 1, s), scalar1=1.0,
                                                  scalar2=None, op0=ALU.add), reads=[b_MOD], writes=[b_MOD])
        P.op(dve, lambda: nc.vector.tensor_copy(out=H0C[:, 1, :, s], in_=modv(0, 0, 0, s)),
             reads=[b_MOD], writes=[b_MOD])
        for l in range(DEPTH):
            for j in range(2):
                if j == 0 and l < NA:
                    P.op(dve, lambda: nc.vector.scalar_tensor_tensor(
                        out=DRV[:, l, j, 1, :, s], in0=modv(l, j, 2, s), scalar=1.0 / ALPHA, in1=PSC[:, l, :],
                        op0=ALU.mult, op1=ALU.mult), reads=[b_MOD, b_const], writes=[b_MOD])
                else:
                    P.op(dve, lambda: nc.vector.tensor_scalar(
                        out=DRV[:, l, j, 1, :, s], in0=modv(l, j, 2, s), scalar1=1.0 / ALPHA, scalar2=None,
                        op0=ALU.mult), reads=[b_MOD], writes=[b_MOD])
                if j == 0:
                    l2, j2 = l, 1
                elif l + 1 < DEPTH:
                    l2, j2 = l + 1, 0
                else:
                    l2 = None
                if l2 is not None:
                    P.op(dve, lambda: nc.vector.tensor_scalar(
                        out=DRV[:, l, j, 0, :, s], in0=modv(l2, j2, 1, s), scalar1=1.0, scalar2=None, op0=ALU.add),
                        reads=[b_MOD], writes=[b_MOD])
                    P.op(dve, lambda: nc.vector.tensor_tensor(
                        out=DRV[:, l, j, 2, :, s], in0=LNG[:, l, j, :], in1=DRV[:, l, j, 0, :, s], op=ALU.mult),
                        reads=[b_MOD, b_const], writes=[b_MOD])
                    P.op(dve, lambda: nc.vector.tensor_tensor(
                        out=DRV[:, l, j, 4, :, s], in0=LNB[:, l, j, :], in1=DRV[:, l, j, 0, :, s], op=ALU.mult),
                        reads=[b_MOD, b_const], writes=[b_MOD])
                    P.op(dve, lambda: nc.vector.tensor_tensor(
                        out=DRV[:, l, j, 3, :, s], in0=DRV[:, l, j, 4, :, s], in1=modv(l2, j2, 0, s), op=ALU.add),
                        reads=[b_MOD], writes=[b_MOD])

    checkpoint("drv", DRV[:, :, :, :, :, :].rearrange("p l j k m s -> p (l j k m s)"), [b_MOD])
    wr_state = [0]

    def wload(idx):
        slot = wr_state[0] % WR_N
        wr_state[0] += 1
        P.dma(sp, c_wr[slot], WR[:, slot, :], wimg[idx, :, :], reads=[b_wimg], writes=[b_WR[slot]])
        return slot

    def mm(bank, col0, col1, lhsT, rhs, start, stop, reads, signal=None):
        if signal is None:
            signal = stop
        P.op(pe, lambda: nc.tensor.matmul(PS[bank][:, col0:col1], lhsT=lhsT, rhs=rhs, start=start, stop=stop),
             reads=reads, writes=[psb[bank]], signal=signal)

    ubi = [0]

    def epilogue_chunk(l, j, s, m, bank, xsrc, xsrc_b):
        P.op(dve, lambda: nc.vector.scalar_tensor_tensor(
            out=U[:, m, :], in0=PS[bank][:, :], scalar=DRV[:, l, j, 1, m:m + 1, s], in1=xsrc[:, m, :],
            op0=ALU.mult, op1=ALU.add), reads=[psb[bank], xsrc_b[m], b_MOD], writes=[b_U[m]])

    def stats_chunk(m, first, last):
        i = ubi[0] % 3
        ubi[0] += 1
        P.op(act, lambda: nc.scalar.activation(out=USQ[:, i, :], in_=U[:, m, :], func=AF.Square),
             reads=[b_U[m]], writes=[b_USQ[i]])
        P.op(pool, lambda: nc.gpsimd.tensor_copy(out=UB[:, i, :], in_=U[:, m, :]),
             reads=[b_U[m]], writes=[b_UB[i]])
        mm(4, 0, T, ONES[:, :], UB[:, i, :], first, last, [b_UB[i], b_ones], signal=True)
        mm(5, 0, T, ONES[:, :], USQ[:, i, :], first, last, [b_USQ[i], b_ones], signal=True)

    def rstd_from(bank_s2, n, eps, mean_slot, with_mean):
        inv = 1.0 / n
        if with_mean:
            P.op(dve, lambda: nc.vector.tensor_scalar(out=ST[:, 0, :], in0=PS[4][:, :], scalar1=inv, scalar2=None,
                                                      op0=ALU.mult), reads=[psb[4]], writes=[b_ST[0]])
            P.op(dve, lambda: nc.vector.tensor_tensor(out=ST[:, 3, :], in0=ST[:, 0, :], in1=ST[:, 0, :],
                                                      op=ALU.mult), reads=[b_ST[0]], writes=[b_ST[3]])
            P.op(dve, lambda: nc.vector.scalar_tensor_tensor(
                out=ST[:, 4, :], in0=PS[bank_s2][:, :], scalar=inv, in1=ST[:, 3, :], op0=ALU.mult,
                op1=ALU.subtract), reads=[psb[bank_s2], b_ST[3]], writes=[b_ST[4]])
            P.op(dve, lambda: nc.vector.tensor_scalar(out=ST[:, 4, :], in0=ST[:, 4, :], scalar1=0.0, scalar2=eps,
                                                      op0=ALU.max, op1=ALU.add), reads=[b_ST[4]], writes=[b_ST[4]])
        else:
            P.op(dve, lambda: nc.vector.tensor_scalar(out=ST[:, 4, :], in0=PS[bank_s2][:, :], scalar1=inv,
                                                      scalar2=eps, op0=ALU.mult, op1=ALU.add),
                 reads=[psb[bank_s2]], writes=[b_ST[4]])
        P.op(act, lambda: nc.scalar.activation(out=ST[:, 5, :], in_=ST[:, 4, :], func=AF.Sqrt),
             reads=[b_ST[4]], writes=[b_ST[5]])
        P.op(dve, lambda: nc.vector.reciprocal(out=ST[:, 1, :], in_=ST[:, 5, :]), reads=[b_ST[5]], writes=[b_ST[1]])
        if with_mean:
            P.op(dve, lambda: nc.vector.scalar_tensor_tensor(
                out=ST[:, 2, :], in0=ST[:, 0, :], scalar=-1.0, in1=ST[:, 1, :], op0=ALU.mult, op1=ALU.mult),
                reads=[b_ST[0], b_ST[1]], writes=[b_ST[2]])

    def ln_finish(l, j, s, last_sub, pool_next):
        rstd_from(5, D, LN_EPS / (ALPHA * ALPHA), 0, True)
        for m in range(NCH):
            P.op(dve, lambda: nc.vector.tensor_tensor(out=U[:, m, :], in0=U[:, m, :], in1=ST[:, 1, :], op=ALU.mult),
                 reads=[b_U[m], b_ST[1]], writes=[b_U[m]])
            P.op(pool, lambda: nc.gpsimd.tensor_tensor(out=U[:, m, :], in0=U[:, m, :], in1=ST[:, 2, :], op=ALU.add),
                 reads=[b_U[m], b_ST[2]], writes=[b_U[m]])
            P.op(act, lambda: nc.scalar.activation(out=X[:, m, :], in_=U[:, m, :], func=AF.Identity,
                                                   bias=LNB[:, l, j, m:m + 1], scale=LNG[:, l, j, m:m + 1]),
                 reads=[b_U[m], b_const], writes=[b_X[m]])
            if not last_sub:
                if pool_next:
                    P.op(act, lambda: nc.scalar.activation(
                        out=HW[:, m, 16:528], in_=U[:, m, :], func=AF.Identity,
                        bias=DRV[:, l, j, 3, m:m + 1, s], scale=DRV[:, l, j, 2, m:m + 1, s]),
                        reads=[b_U[m], b_MOD], writes=[b_HW[m]])
                else:
                    P.op(act, lambda: nc.scalar.activation(
                        out=H[:, m, :], in_=U[:, m, :], func=AF.Identity,
                        bias=DRV[:, l, j, 3, m:m + 1, s], scale=DRV[:, l, j, 2, m:m + 1, s]),
                        reads=[b_U[m], b_MOD], writes=[b_H[m]])

    def sublayer_outputs(l, j, s, xsrc, xsrc_b, produce):
        pend = []
        for m in range(NCH):
            bank = produce(m)
            epilogue_chunk(l, j, s, m, bank, xsrc, xsrc_b)
            pend.append(m)
            if len(pend) > 1:
                mmm = pend.pop(0)
                stats_chunk(mmm, mmm == 0, False)
        while pend:
            mmm = pend.pop(0)
            stats_chunk(mmm, mmm == 0, mmm == NCH - 1)

    def pool_mixer(l, s, first_tile, xsrc, xsrc_b):
        wslot = wload(wt_index(l, "pool"))
        Wp = WR[:, wslot, :]
        for m in range(NCH):
            eng = dve if m % 2 == 0 else pool
            P.op(eng, lambda: eng.h.tensor_copy(out=HW[:, m, 0:16], in_=HALO[:, l, m, :]),
                 reads=[b_HALO[l][m]], writes=[b_HW[m]])
        for m in range(NCH):
            g = m // 2
            w = 2 << g
            eng = dve if m % 2 == 0 else pool
            t0i = (m % 2) * 2
            src = HW[:, m, :]
            srcb = b_HW[m]
            sh = 1
            lo = 16 - (w - 1)
            k = 0
            while sh < w:
                lo2 = lo + sh
                dst = PTMP[:, t0i + (k % 2), :]
                dstb = b_PTMP[t0i + (k % 2)]
                P.op(eng, lambda: eng.h.tensor_tensor(out=dst[:, lo2:528], in0=src[:, lo2:528],
                                                      in1=src[:, lo2 - sh:528 - sh], op=ALU.add),
                     reads=[srcb], writes=[dstb])
                src, srcb = dst, dstb
                lo = lo2
                sh *= 2
                k += 1
            P.op(dve, lambda: nc.vector.scalar_tensor_tensor(out=H[:, m, :], in0=src[:, 16:528], scalar=1.0 / w,
                                                             in1=HW[:, m, 16:528], op0=ALU.mult, op1=ALU.subtract),
                 reads=[srcb, b_HW[m]], writes=[b_H[m]])
            if first_tile:
                P.op(eng, lambda: eng.h.tensor_tensor(out=src[:, 16:16 + w - 1], in0=src[:, 16:16 + w - 1],
                                                      in1=CST[:, 16:16 + w - 1], op=ALU.mult),
                     reads=[srcb, b_const], writes=[srcb])
                P.op(eng, lambda: eng.h.tensor_tensor(out=H[:, m, 0:w - 1], in0=src[:, 16:16 + w - 1],
                                                      in1=HW[:, m, 16:16 + w - 1], op=ALU.subtract),
                     reads=[srcb, b_HW[m]], writes=[b_H[m]])
            P.op(eng, lambda: eng.h.tensor_copy(out=HALO[:, l, m, :], in_=HW[:, m, 512:528]),
                 reads=[b_HW[m]], writes=[b_HALO[l][m]])

        def produce(m):
            g, mo = m // 2, m % 2
            bank = rot()
            for kc in range(2):
                off = g * 512 + kc * 256 + mo * 128
                mm(bank, 0, T, Wp[:, off:off + 128], H[:, 2 * g + kc, :], kc == 0, kc == 1,
                   [b_WR[wslot], b_H[2 * g + kc]])
            return bank

        sublayer_outputs(l, 0, s, xsrc, xsrc_b, produce)

    def mlp(l, s):
        for g in range(8):
            wslot = wload(wt_index(l, "w1", g))
            for jj in range(4):
                bank = rot()
                for kc in range(8):
                    mm(bank, 0, T, WR[:, wslot, kc * 512 + jj * 128: kc * 512 + (jj + 1) * 128], H[:, kc, :],
                       kc == 0, kc == 7, [b_WR[wslot], b_H[kc]])
                i = ubi[0] % 3
                ubi[0] += 1
                f = g * 4 + jj
                P.op(act, lambda: nc.scalar.activation(out=RT[:, i, :], in_=PS[bank][:, :], func=AF.Relu),
                     reads=[psb[bank]], writes=[b_RT[i]])
                P.op(pool, lambda: nc.gpsimd.tensor_tensor(out=A[:, f, :], in0=RT[:, i, :], in1=RT[:, i, :],
                                                           op=ALU.mult), reads=[b_RT[i]], writes=[b_A[f]])
        first_stat = [True]
        pend = []

        def flush_stats(n, final):
            k = 0
            while pend and k < n:
                mmm = pend.pop(0)
                stats_chunk(mmm, mmm == 0, mmm == NCH - 1)
                k += 1

        for half in range(2):
            banks = [0, 1, 2, 3] if half == 0 else [6, 7, 2, 3]
            if half == 1:
                banks = [6, 7, 0, 1]
            for jt in range(4):
                wslot = wload(wt_index(l, "w2", half * 4 + jt))
                for fl in range(8):
                    ffc = jt * 8 + fl
                    for q in range(4):
                        mm(banks[q], 0, T, WR[:, wslot, fl * 512 + q * 128: fl * 512 + (q + 1) * 128], A[:, ffc, :],
                           ffc == 0, ffc == 31, [b_WR[wslot], b_A[ffc]], signal=(ffc == 31 or (fl == 7 and q == 3)))
                if half == 1 and jt == 1:
                    flush_stats(4, False)
            for q in range(4):
                m = half * 4 + q
                epilogue_chunk(l, 1, s, m, banks[q], X, b_X)
                pend.append(m)
        flush_stats(8, True)

    def rope_tables(s, t0):
        src = bass.AP(pos_t, s * S + t0, [[0, 128], [1, T]])
        P.dma(sp, c_pos, POSI[:, :], src, writes=[b_POSI])
        P.op(dve, lambda: nc.vector.tensor_copy(out=TR[:, 0, :], in_=POSI[:, :]), reads=[b_POSI], writes=[b_TR[0]])
        P.op(dve, lambda: nc.vector.tensor_scalar(out=TR[:, 0, :], in0=TR[:, 0, :], scalar1=CST[:, 0:1], scalar2=None,
                                                  op0=ALU.mult), reads=[b_TR[0], b_const], writes=[b_TR[0]])
        a_ap, a_b = TR[:, 0, :], b_TR[0]
        P.op(dve, lambda: nc.vector.tensor_scalar(out=KI[:, :], in0=a_ap, scalar1=1.0 / TWO_PI, scalar2=None,
                                                  op0=ALU.mult), reads=[a_b], writes=[b_KI])
        P.op(dve, lambda: nc.vector.tensor_copy(out=TR[:, 2, :], in_=KI[:, :]), reads=[b_KI], writes=[b_TR[2]])
        P.op(dve, lambda: nc.vector.scalar_tensor_tensor(out=TR[:, 3, :], in0=TR[:, 2, :], scalar=-C1, in1=a_ap,
                                                         op0=ALU.mult, op1=ALU.add),
             reads=[b_TR[2], a_b], writes=[b_TR[3]])
        P.op(dve, lambda: nc.vector.scalar_tensor_tensor(out=TR[:, 3, :], in0=TR[:, 2, :], scalar=-C2,
                                                         in1=TR[:, 3, :], op0=ALU.mult, op1=ALU.add),
             reads=[b_TR[2], b_TR[3]], writes=[b_TR[3]])
        LIM = math.pi - 2e-6
        P.op(dve, lambda: nc.vector.tensor_scalar(out=TR[:, 3, :], in0=TR[:, 3, :], scalar1=LIM,
                                                  scalar2=-LIM, op0=ALU.min, op1=ALU.max),
             reads=[b_TR[3]], writes=[b_TR[3]])
        P.op(act, lambda: nc.scalar.activation(out=CS[:, 1, :], in_=TR[:, 3, :], func=AF.Sin,
                                               scale=CST[:, 1:2]), reads=[b_TR[3], b_const], writes=[b_CS])
        P.op(dve, lambda: nc.vector.tensor_scalar(out=TR[:, 1, :], in0=TR[:, 3, :], scalar1=math.pi / 2,
                                                  scalar2=None, op0=ALU.add), reads=[b_TR[3]], writes=[b_TR[1]])
        P.op(dve, lambda: nc.vector.tensor_scalar(out=TR[:, 2, :], in0=TR[:, 1, :], scalar1=math.pi,
                                                  scalar2=-TWO_PI, op0=ALU.is_gt, op1=ALU.mult),
             reads=[b_TR[1]], writes=[b_TR[2]])
        P.op(dve, lambda: nc.vector.tensor_tensor(out=TR[:, 1, :], in0=TR[:, 1, :], in1=TR[:, 2, :], op=ALU.add),
             reads=[b_TR[1], b_TR[2]], writes=[b_TR[1]])
        P.op(dve, lambda: nc.vector.tensor_scalar(out=TR[:, 1, :], in0=TR[:, 1, :], scalar1=LIM,
                                                  scalar2=-LIM, op0=ALU.min, op1=ALU.max),
             reads=[b_TR[1]], writes=[b_TR[1]])
        P.op(act, lambda: nc.scalar.activation(out=CS[:, 0, :], in_=TR[:, 1, :], func=AF.Sin),
             reads=[b_TR[1]], writes=[b_CS])

    def rope_apply(bankA, bankB, out_ap, out_bufs, qpair=None):
        P.op(dve, lambda: nc.vector.tensor_tensor(out=TR[:, 0, :], in0=PS[bankA][:, :], in1=CS[:, 0, :], op=ALU.mult),
             reads=[psb[bankA], b_CS], writes=[b_TR[0]])
        P.op(dve, lambda: nc.vector.tensor_tensor(out=TR[:, 1, :], in0=PS[bankB][:, :], in1=CS[:, 1, :], op=ALU.mult),
             reads=[psb[bankB], b_CS], writes=[b_TR[1]])
        if qpair is None:
            P.op(pool, lambda: nc.gpsimd.tensor_tensor(out=out_ap, in0=TR[:, 0, :], in1=TR[:, 1, :], op=ALU.add),
                 reads=[b_TR[0], b_TR[1]], writes=out_bufs)
        else:
            for hh in range(2):
                p0 = hh * 64
                P.op(pool, lambda: nc.gpsimd.tensor_tensor(out=QRz[p0:p0 + 64, 2 * qpair + hh, :],
                                                           in0=TR[p0:p0 + 64, 0, :], in1=TR[p0:p0 + 64, 1, :],
                                                           op=ALU.add),
                     reads=[b_TR[0], b_TR[1]], writes=[b_QR[2 * qpair + hh]])

    def rms_chunks(nch, srcs_bank_fn, g_ap_fn, out_ap_fn, out_bufs, eps, n):
        for mc in range(nch):
            bank = srcs_bank_fn(mc)
            P.op(act, lambda: nc.scalar.copy(out=C32[:, mc, :], in_=PS[bank][:, :]),
                 reads=[psb[bank]], writes=[b_C32[mc]])
            i = ubi[0] % 3
            ubi[0] += 1
            P.op(act, lambda: nc.scalar.activation(out=USQ[:, i, :], in_=C32[:, mc, :], func=AF.Square),
                 reads=[b_C32[mc]], writes=[b_USQ[i]])
            mm(5, 0, T, ONES[:, :], USQ[:, i, :], mc == 0, mc == nch - 1, [b_USQ[i], b_ones], signal=True)
        rstd_from(5, n, eps, 0, False)
        for mc in range(nch):
            P.op(dve, lambda: nc.vector.scalar_tensor_tensor(
                out=out_ap_fn(mc), in0=C32[:, mc, :], scalar=g_ap_fn(mc), in1=ST[:, 1, :], op0=ALU.mult,
                op1=ALU.mult), reads=[b_C32[mc], b_ST[1], b_const], writes=[out_bufs[mc]])

    def kv_phase(s, ti):
        t0 = ti * T
        for m in range(NCH):
            P.op(pool, lambda: nc.gpsimd.tensor_copy(out=A[:, 8 + m, :], in_=X[:, m, :]),
                 reads=[b_X[m]], writes=[b_A[8 + m]])
        ws = wload(wt_index(0, "kvin"))

        def proj(mc):
            bank = rot()
            for kc in range(8):
                mm(bank, 0, T, WR[:, ws, kc * 512 + mc * 128: kc * 512 + (mc + 1) * 128], A[:, 8 + kc, :],
                   kc == 0, kc == 7, [b_WR[ws], b_A[8 + kc]])
            return bank

        rms_chunks(2, proj, lambda mc: KVG[:, mc:mc + 1], lambda mc: CKVN[:, mc, :], b_CKVN, RMS_EPS, KVR)
        bA = proj(2)
        bB = proj(3)
        rope_apply(bA, bB, KRS[:, t0:t0 + T], [b_KRS[ti]])
        ws2 = wload(wt_index(0, "kvup"))
        for h in range(NH):
            bank = rot()
            for kc in range(2):
                mm(bank, 0, T, WR[:, ws2, kc * 1024 + h * 128: kc * 1024 + (h + 1) * 128], CKVN[:, kc, :],
                   kc == 0, kc == 1, [b_WR[ws2], b_CKVN[kc]])
            eng = act if h % 2 == 0 else dve
            if h % 2 == 0:
                P.op(act, lambda: nc.scalar.copy(out=A[:, 16 + h, :], in_=PS[bank][:, :]),
                     reads=[psb[bank]], writes=[b_A[16 + h]])
            else:
                P.op(dve, lambda: nc.vector.tensor_copy(out=A[:, 16 + h, :], in_=PS[bank][:, :]),
                     reads=[psb[bank]], writes=[b_A[16 + h]])
        P.dma(pool, c_kst, kT_d[s, :, :, t0:t0 + T].rearrange("h p t -> p h t"), A[:, 16:24, :],
              reads=b_A[16:24], writes=[b_kd[s][ti]])
        for ks in range(4):
            for hg in range(2):
                bank = rot()
                for kc in range(2):
                    mm(bank, 0, T, CKVN[:, kc, ks * 128:(ks + 1) * 128],
                       WR[:, ws2, 2048 + kc * 1024 + hg * 512: 2048 + kc * 1024 + (hg + 1) * 512],
                       kc == 0, kc == 1, [b_WR[ws2], b_CKVN[kc]])
                o = A[:, 24 + hg * 4: 24 + hg * 4 + 4, ks * 128:(ks + 1) * 128]
                i_ = PS[bank][:, :].rearrange("p (h d) -> p h d", d=128)
                if (ks + hg) % 2 == 0:
                    P.op(act, lambda: nc.scalar.copy(out=o, in_=i_), reads=[psb[bank]],
                         writes=b_A[24 + hg * 4: 24 + hg * 4 + 4])
                else:
                    P.op(dve, lambda: nc.vector.tensor_copy(out=o, in_=i_), reads=[psb[bank]],
                         writes=b_A[24 + hg * 4: 24 + hg * 4 + 4])
        P.dma(pool, c_vst, v_d[s, :, :, ti * 4:(ti + 1) * 4, :].rearrange("h p k d -> p h (k d)"), A[:, 24:32, :],
              reads=b_A[24:32], writes=[b_vd[s][ti]])

    kb_state = [0]

    def attention(l, s, ti):
        jl = l - NA
        ws = wload(wt_index(l, "qd"))

        def projq(mc):
            bank = rot()
            for kc in range(8):
                mm(bank, 0, T, WR[:, ws, kc * 512 + mc * 128: kc * 512 + (mc + 1) * 128], H[:, kc, :],
                   kc == 0, kc == 7, [b_WR[ws], b_H[kc]])
            return bank

        rms_chunks(4, projq, lambda mc: QNG[:, jl, mc:mc + 1], lambda mc: CQN[:, mc, :], b_CQN, RMS_EPS, QR)
        wa = wload(wt_index(l, "quA"))
        for h in range(NH):
            bank = rot()
            for kc in range(4):
                mm(bank, 0, T, WR[:, wa, kc * 1024 + h * 128: kc * 1024 + (h + 1) * 128], CQN[:, kc, :],
                   kc == 0, kc == 3, [b_WR[wa], b_CQN[kc]])
            if h % 2 == 0:
                P.op(act, lambda: nc.scalar.copy(out=A[:, h, :], in_=PS[bank][:, :]),
                     reads=[psb[bank]], writes=[b_A[h]])
            else:
                P.op(dve, lambda: nc.vector.tensor_copy(out=A[:, h, :], in_=PS[bank][:, :]),
                     reads=[psb[bank]], writes=[b_A[h]])
        wb = wload(wt_index(l, "quB"))
        for pp in range(4):
            bA = rot()
            for kc in range(4):
                mm(bA, 0, T, WR[:, wb, kc * 1024 + pp * 128: kc * 1024 + (pp + 1) * 128], CQN[:, kc, :],
                   kc == 0, kc == 3, [b_WR[wb], b_CQN[kc]])
            bB = rot()
            for kc in range(4):
                mm(bB, 0, T, WR[:, wb, kc * 1024 + 512 + pp * 128: kc * 1024 + 512 + (pp + 1) * 128], CQN[:, kc, :],
                   kc == 0, kc == 3, [b_WR[wb], b_CQN[kc]])
            rope_apply(bA, bB, None, None, qpair=pp)
        subs = []
        for h in range(NH):
            n_units = (ti + 1) * 4
            u_i = 0
            for kb in range(ti + 1):
                for ks in range(4):
                    subs.append(dict(h=h, kb=kb, ks=ks, diag=(kb == ti), first=(u_i == 0), last=(u_i == n_units - 1)))
                    u_i += 1
        slot_of = {}
        pti = [0]
        LA = 2

        def emit_s_exp(sd):
            h, kb, ks = sd["h"], sd["kb"], sd["ks"]
            if ks == 0:
                slot = kb_state[0] % KB_N
                kb_state[0] += 1
                slot_of[(h, kb)] = slot
                P.dma(sp, c_kb[slot], KBLK[:, slot, :], kT_d[s, h, :, kb * T:(kb + 1) * T],
                      reads=[b_kd[s][kb]], writes=[b_KBLK[slot]])
                P.dma(sp, c_vb[slot], VBLK[:, slot, :, :], v_d[s, h, :, kb * 4:(kb + 1) * 4, :],
                      reads=[b_vd[s][kb]], writes=[b_VBLK[slot]])
            slot = slot_of[(h, kb)]
            c0 = ks * 128 if sd["diag"] else 0
            sbank = rot()
            mm(sbank, c0, T, KBLK[:, slot, ks * 128:(ks + 1) * 128], A[:, h, c0:T], True, False,
               [b_KBLK[slot], b_A[h]], signal=False)
            kcol = kb * T + ks * 128
            mm(sbank, c0, T, KRS[:, kcol:kcol + 128], QRz[:, h, c0:T], False, True,
               [b_KRS[kb], b_QR[h]])
            pi = pti[0] % PT_N
            pti[0] += 1
            sd["pi"] = pi
            sd["c0"] = c0
            sd["slot"] = slot
            P.op(act, lambda: nc.scalar.activation(out=PT[:, pi, c0:T], in_=PS[sbank][:, c0:T], func=AF.Exp,
                                                   scale=ATTN_SCALE),
                 reads=[psb[sbank]], writes=[b_PT[pi]])
            if sd["diag"]:
                P.op(pool, lambda: nc.gpsimd.tensor_tensor(out=PT[:, pi, c0:c0 + 128],
                                                           in0=PT[:, pi, c0:c0 + 128], in1=TRI[:, :],
                                                           op=ALU.mult),
                     reads=[b_PT[pi], b_const], writes=[b_PT[pi]])

        def emit_pv(sd):
            h, ks, pi, c0, slot = sd["h"], sd["ks"], sd["pi"], sd["c0"], sd["slot"]
            ob = 4 + (h % 2)
            lb = 6 + (h % 2)
            mm(ob, c0, T, VBLK[:, slot, ks, :], PT[:, pi, c0:T], sd["first"], sd["last"], [b_VBLK[slot], b_PT[pi]])
            mm(lb, c0, T, ONES[:, :], PT[:, pi, c0:T], sd["first"], sd["last"], [b_PT[pi], b_ones], signal=True)
            if sd["last"]:
                si = 3 + (h % 2)
                P.op(dve, lambda: nc.vector.reciprocal(out=ST[:, si, :], in_=PS[lb][:, :]),
                     reads=[psb[lb]], writes=[b_ST[si]])
                P.op(dve, lambda: nc.vector.tensor_tensor(out=A[:, 8 + h, :], in0=PS[ob][:, :], in1=ST[:, si, :],
                                                          op=ALU.mult),
                     reads=[psb[ob], b_ST[si]], writes=[b_A[8 + h]])

        for j in range(len(subs) + LA):
            if j < len(subs):
                emit_s_exp(subs[j])
            if j >= LA:
                emit_pv(subs[j - LA])
        wos = [None, None]

        def produce(m):
            half = m // 4
            if wos[half] is None:
                wos[half] = wload(wt_index(l, "wo", half))
            wsl = wos[half]
            bank = rot()
            for h in range(NH):
                mm(bank, 0, T, WR[:, wsl, h * 512 + (m % 4) * 128: h * 512 + (m % 4 + 1) * 128], A[:, 8 + h, :],
                   h == 0, h == NH - 1, [b_WR[wsl], b_A[8 + h]])
            return bank

        sublayer_outputs(l, 0, s, X, b_X, produce)

    order = [(s, ti) for s in range(NSEQ) for ti in range(NTILE)]

    def load_x(s, ti):
        P.dma(sp, c_xin, xin[:, :, :], xT[s, :, ti * T:(ti + 1) * T].rearrange("(c p) t -> p c t", p=128),
              writes=b_xin)

    load_x(*order[0])
    for oi, (s, ti) in enumerate(order):
        if ti == 0 and s > 0:
            P.op(dve, lambda: nc.vector.memset(HALO[:, :, :, :], 0.0), writes=[b for l in b_HALO for b in l])
        rope_tables(s, ti * T)
        if oi == 0:
            checkpoint("rope", CS[:, :, :].rearrange("p a t -> p (a t)"), [b_CS])
        for m in range(NCH):
            P.op(act, lambda: nc.scalar.activation(out=HW[:, m, 16:528], in_=xin[:, m, :], func=AF.Identity,
                                                   bias=H0C[:, 1, m:m + 1, s], scale=H0C[:, 0, m:m + 1, s]),
                 reads=[b_xin[m], b_MOD], writes=[b_HW[m]])
        for l in range(DEPTH):
            xsrc, xsrc_b = (xin, b_xin) if l == 0 else (X, b_X)
            if l < NA:
                pool_mixer(l, s, ti == 0, xsrc, xsrc_b)
            else:
                attention(l, s, ti)
            ln_finish(l, 0, s, False, False)
            if oi == 0:
                checkpoint(f"x{l}0", X[:, :, :].rearrange("p c t -> p (c t)"), b_X)
            if l == 0 and oi + 1 < len(order):
                load_x(*order[oi + 1])
            mlp(l, s)
            last = l == DEPTH - 1
            ln_finish(l, 1, s, last, (l + 1) < NA)
            if oi == 0:
                checkpoint(f"x{l}1", X[:, :, :].rearrange("p c t -> p (c t)"), b_X)
            if l == NA - 1:
                kv_phase(s, ti)
                if oi == 0:
                    checkpoint("kv")
        P.dma(pool, c_out, outT[s, :, ti * T:(ti + 1) * T].rearrange("(c p) t -> p c t", p=128), X[:, :, :],
              reads=b_X, writes=[b_out])

    assert P.check_deadlock(), "deadlock in generated program"
    P.wait_all(pool, [c_out, c_kst, c_vst, c_cvt])
    P.wait_all(sp, c_kb + c_vb + c_wr + [c_xin, c_pos, c_const, c_dbg] + c_ada)
    es.close()
    return nc


def make_core_inputs(inp, seqs, S):
    x = inp["x"][seqs, :S]
    m = {}
    m["xT"] = np.ascontiguousarray(np.transpose(x, (0, 2, 1)))
    m["pos"] = np.ascontiguousarray(inp["positions"][seqs, :S]).astype(np.int32)
    m["cT"] = np.ascontiguousarray(np.transpose(fm(inp["c"][seqs]), (0, 2, 1)))
    return m


def make_shared_inputs(inp):
    m = {}
    m["adaw"] = build_ada_image(np.asarray(inp["ada_w"], np.float32))
    m["adab"] = fm(inp["ada_b"])
    m["lng"] = fm(inp["ln_g"])
    m["lnb"] = fm(inp["ln_b"])
    m["psc"] = fm(inp["pool_scale"])
    m["qng"] = fm(inp["q_norm_g"])
    m["kvg"] = fm(inp["kv_norm_g"])
    cst = np.zeros((128, 32), np.float32)
    inv_freq = (10000.0 ** (-np.arange(0, 64, 2, dtype=np.float32) / np.float32(64))).astype(np.float32)
    p = np.arange(128)
    cst[:, 0] = inv_freq[p % 32]
    cst[:, 1] = np.where((p % 64) < 32, -1.0, 1.0)
    cst[:, 16:32] = 1.0 / (np.arange(16, dtype=np.float32) + 1.0)
    m["cst"] = cst
    k = np.arange(128)[:, None]
    q = np.arange(128)[None, :]
    m["tri"] = (k <= q).astype(np.float32).astype(ml_dtypes.bfloat16)
    m["wimg32"] = build_weight_image(inp)
    return m


_NC_CACHE = {}


def run(inp, n_cores, NSEQ, S, trace=False, stop=None):
    inp = {k: np.asarray(v) for k, v in inp.items()}
    shared = make_shared_inputs(inp)
    in_maps = []
    for c in range(n_cores):
        seqs = list(range(c * NSEQ, (c + 1) * NSEQ))
        m = dict(shared)
        m.update(make_core_inputs(inp, seqs, S))
        in_maps.append(m)
    key = (NSEQ, S)
    nc = build_program(NSEQ, S, stop)
    res = run_bass_kernel_spmd(nc, in_maps, core_ids=list(range(n_cores)), trace=trace)
    outs = [np.transpose(r["outT"], (0, 2, 1)) for r in res.results]
    return np.ascontiguousarray(np.concatenate(outs, axis=0)).astype(np.float32), res


def kernel(**inputs):
    out, _ = run(inputs, 8, 2, 4096)
    return out
```

```python
import math
from contextlib import ExitStack

import numpy as np
import ml_dtypes
import concourse.bass as bass
import concourse.mybir as mybir
from concourse.bass_utils import run_bass_kernel_spmd

F32 = mybir.dt.float32
BF16 = mybir.dt.bfloat16
I32 = mybir.dt.int32
ALU = mybir.AluOpType
AF = mybir.ActivationFunctionType

D = 1024
NCH = 8
DEPTH = 4
NA = 2
DFF = 4096
NH = 8
QR = 512
KVR = 256
T = 512
ALPHA = (2.0 * DEPTH) ** 0.25
LN_EPS = 1e-5
RMS_EPS = 1e-6
ATTN_SCALE = (128 + 64) ** -0.5
N_WT_POOL = 17
N_WT_ATT = 21
TWO_PI = 2.0 * math.pi
C1 = 6.28125
C2 = TWO_PI - C1


class Ctr:
    def __init__(self, sem, name):
        self.sem = sem
        self.name = name
        self.count = 0


class Buf:
    __slots__ = ("name", "lw", "rd")

    def __init__(self, name):
        self.name = name
        self.lw = None
        self.rd = {}


class Eng:
    def __init__(self, name, h, ctr, is_pe=False):
        self.name = name
        self.h = h
        self.ctr = ctr
        self.waited = {}
        self.is_pe = is_pe


def _flat(lst):
    out = []
    for b in lst:
        if isinstance(b, (list, tuple)):
            out.extend(_flat(b))
        else:
            out.append(b)
    return out


class Prog:
    def __init__(self, nc, es):
        self.nc = nc
        self.es = es
        self.nsem = 0
        self.stopped = False
        self.log = {}
        self.pend = {}

        def mk(name, h, is_pe=False):
            return Eng(name, h, self.new_ctr(name), is_pe)

        self.pe = mk("pe", nc.tensor, True)
        self.act = mk("act", nc.scalar)
        self.dve = mk("dve", nc.vector)
        self.pool = mk("pool", nc.gpsimd)
        self.sp = mk("sp", nc.sync)

    def new_ctr(self, name):
        sem = self.es.enter_context(self.nc.semaphore("s_" + name))
        self.nsem += 1
        return Ctr(sem, name)

    def _deps(self, eng, reads, writes, own_ctr):
        deps = {}

        def add(c, v):
            if deps.get(c, 0) < v:
                deps[c] = v

        for b in reads:
            if b.lw is not None:
                add(*b.lw)
        for b in writes:
            if b.lw is not None:
                add(*b.lw)
            for c, v in b.rd.items():
                add(c, v)
        for c, v in deps.items():
            if c is own_ctr and eng.is_pe:
                continue
            if eng.waited.get(c, 0) >= v:
                continue
            if c is own_ctr and v > c.count:
                continue
            eng.h.wait_ge(c.sem, v)
            eng.waited[c] = v

    def op(self, eng, fn, reads=(), writes=(), signal=True):
        if self.stopped:
            return None
        c = eng.ctr
        reads = _flat(reads)
        writes = _flat(writes)
        self._deps_compute(eng, reads, writes)
        ins = fn()
        self.log.setdefault(eng.name, []).append((self.pend.pop(eng.name, []), c if signal else None, 1, "op"))
        if signal:
            c.count += 1
            ins.then_inc(c.sem, 1)
            v = c.count
        else:
            v = c.count + 1
        for b in writes:
            b.lw = (c, v)
            b.rd = {}
        for b in reads:
            if b.rd.get(c, 0) < v:
                b.rd[c] = v
        return ins

    def _deps_compute(self, eng, reads, writes):
        own = eng.ctr
        deps = {}

        def add(c, v):
            if deps.get(c, 0) < v:
                deps[c] = v

        for b in reads:
            if b.lw is not None:
                add(*b.lw)
        for b in writes:
            if b.lw is not None and b.lw[0] is not own:
                add(*b.lw)
            for c, v in b.rd.items():
                if c is not own:
                    add(c, v)
        for c, v in deps.items():
            if c is own:
                if eng.is_pe:
                    continue
                if v > c.count:
                    continue
            if eng.waited.get(c, 0) >= v:
                continue
            eng.h.wait_ge(c.sem, v)
            eng.waited[c] = v
            self.pend.setdefault(eng.name, []).append((c, v))

    def dma(self, eng, ctr, out, in_, reads=(), writes=(), **kw):
        if self.stopped:
            return None
        reads = _flat(reads)
        writes = _flat(writes)
        deps = {}

        def add(c, v):
            if deps.get(c, 0) < v:
                deps[c] = v

        for b in reads:
            if b.lw is not None:
                add(*b.lw)
        for b in writes:
            if b.lw is not None:
                add(*b.lw)
            for c, v in b.rd.items():
                add(c, v)
        for c, v in deps.items():
            if eng.waited.get(c, 0) >= v:
                continue
            eng.h.wait_ge(c.sem, v)
            eng.waited[c] = v
            self.pend.setdefault(eng.name, []).append((c, v))
        ins = eng.h.dma_start(out=out, in_=in_, **kw)
        self.log.setdefault(eng.name, []).append((self.pend.pop(eng.name, []), ctr, 16, "dma"))
        ctr.count += 16
        ins.then_inc(ctr.sem, 16)
        v = ctr.count
        for b in writes:
            b.lw = (ctr, v)
            b.rd = {}
        for b in reads:
            if b.rd.get(ctr, 0) < v:
                b.rd[ctr] = v
        return ins

    def check_deadlock(self):
        val = {}
        pos = {k: 0 for k in self.log}
        progress = True
        while progress:
            progress = False
            for k, lst in self.log.items():
                while pos[k] < len(lst):
                    waits, c, inc, kind = lst[pos[k]]
                    if all(val.get(cc, 0) >= vv for cc, vv in waits):
                        if c is not None:
                            val[c] = val.get(c, 0) + inc
                        pos[k] += 1
                        progress = True
                    else:
                        break
        stuck = {k: pos[k] for k in self.log if pos[k] < len(self.log[k])}
        for k, p in stuck.items():
            waits, c, inc, kind = self.log[k][p]
            print("DEADLOCK: engine", k, "instr", p, "/", len(self.log[k]), kind, "waits",
                  [(cc.name, vv, val.get(cc, 0)) for cc, vv in waits])
        return not stuck

    def wait_all(self, eng, ctrs):
        for c in ctrs:
            if c.count > 0 and eng.waited.get(c, 0) < c.count:
                eng.h.wait_ge(c.sem, c.count)
                eng.waited[c] = c.count


def _wtile_from_rows(w, kcs, cols):
    K = w.shape[0]
    assert K == kcs * 128
    sub = w[:, cols]
    return np.ascontiguousarray(sub.reshape(kcs, 128, len(cols)).transpose(1, 0, 2).reshape(128, -1))


def build_weight_image(inp):
    tiles = []

    def pad(t):
        out = np.zeros((128, 4096), np.float32)
        out[:, : t.shape[1]] = t
        return out

    def mlp_tiles(l):
        w1 = inp["mlp_w1"][l]
        w2 = inp["mlp_w2"][l]
        for g in range(8):
            tiles.append(pad(_wtile_from_rows(w1, 8, np.arange(g * 512, (g + 1) * 512))))
        for half in range(2):
            for jt in range(4):
                rows = w2[jt * 1024:(jt + 1) * 1024, half * 512:(half + 1) * 512]
                tiles.append(pad(_wtile_from_rows(rows, 8, np.arange(512))))

    rope_perm = np.concatenate([np.arange(32, 64), np.arange(0, 32)])
    for l in range(DEPTH):
        if l < NA:
            pw = inp["pool_w"][l]
            t = np.concatenate([_wtile_from_rows(pw[g], 2, np.arange(256)) for g in range(4)], axis=1)
            tiles.append(pad(t))
        else:
            j = l - NA
            tiles.append(pad(_wtile_from_rows(inp["q_down_w"][j], 8, np.arange(512))))
            qu = inp["q_up_w"][j]
            nope_cols = np.concatenate([np.arange(h * 192, h * 192 + 128) for h in range(NH)])
            tiles.append(pad(_wtile_from_rows(qu, 4, nope_cols)))
            rope_cols = np.concatenate([np.arange(h * 192 + 128, h * 192 + 192) for h in range(NH)])
            rot_cols = np.concatenate([h * 192 + 128 + rope_perm for h in range(NH)])
            tiles.append(pad(_wtile_from_rows(qu, 4, np.concatenate([rope_cols, rot_cols]))))
            wo = inp["attn_out_w"][j]
            for half in range(2):
                tiles.append(pad(_wtile_from_rows(wo, 8, np.arange(half * 512, (half + 1) * 512))))
        mlp_tiles(l)
    kvw = inp["kv_in_w"]
    kr = np.arange(256, 320)
    kcols = np.concatenate([np.arange(256), kr, kr, 256 + rope_perm, 256 + rope_perm])
    tiles.append(pad(_wtile_from_rows(kvw, 8, kcols)))
    t = np.concatenate([_wtile_from_rows(inp["k_up_w"], 2, np.arange(1024)),
                        _wtile_from_rows(inp["v_up_w"], 2, np.arange(1024))], axis=1)
    tiles.append(pad(t))
    return np.stack(tiles, 0)


def wt_index(l, which, i=0):
    base = 0
    for ll in range(l):
        base += N_WT_POOL if ll < NA else N_WT_ATT
    if which == "kvin":
        return NA * N_WT_POOL + (DEPTH - NA) * N_WT_ATT
    if which == "kvup":
        return NA * N_WT_POOL + (DEPTH - NA) * N_WT_ATT + 1
    if l < NA:
        off = {"pool": 0, "w1": 1, "w2": 9}[which]
    else:
        off = {"qd": 0, "quA": 1, "quB": 2, "wo": 3, "w1": 5, "w2": 13}[which]
    return base + off + i


NT_W = NA * N_WT_POOL + (DEPTH - NA) * N_WT_ATT + 2


def fm(v):
    v = np.asarray(v, np.float32)
    lead = v.shape[:-1]
    n = v.shape[-1] // 128
    r = v.reshape(*lead, n, 128)
    return np.ascontiguousarray(np.moveaxis(r, -1, 0))


def build_ada_image(ada_w):
    a = ada_w.reshape(DEPTH, 8, 128, 12, 512)
    return np.ascontiguousarray(a.transpose(0, 3, 2, 1, 4))


class _Stop(Exception):
    pass


def build_program(NSEQ, S, stop=None):
    NTILE = S // T
    NKB = S // T
    nc = bass.Bass("TRN2", target_bir_lowering=False)
    es = ExitStack()

    def din(name, shape, dt=F32):
        return nc.dram_tensor(name, list(shape), dt, kind="ExternalInput").ap()

    xT = din("xT", [NSEQ, D, S])
    pos_t = nc.dram_tensor("pos", [NSEQ, S], I32, kind="ExternalInput")
    cT_d = din("cT", [128, NCH, NSEQ])
    adaw_d = din("adaw", [DEPTH, 12, 128, 8, 512])
    adab_d = din("adab", [128, DEPTH, 48])
    lng_d = din("lng", [128, DEPTH, 2, NCH])
    lnb_d = din("lnb", [128, DEPTH, 2, NCH])
    psc_d = din("psc", [128, NA, NCH])
    qng_d = din("qng", [128, DEPTH - NA, 4])
    kvg_d = din("kvg", [128, 2])
    cst_d = din("cst", [128, 32])
    tri_d = din("tri", [128, 128], BF16)
    wimg32 = din("wimg32", [NT_W, 128, 4096])
    outT = nc.dram_tensor("outT", [NSEQ, D, S], F32, kind="ExternalOutput").ap()
    dbg_d = nc.dram_tensor("dbg", [128, 4096], F32, kind="ExternalOutput").ap() if stop else None
    wimg = nc.dram_tensor("wimg", [NT_W, 128, 4096], BF16, kind="Internal").ap()
    kT_d = nc.dram_tensor("kT_d", [NSEQ, NH, 128, S], BF16, kind="Internal").ap()
    v_d = nc.dram_tensor("v_d", [NSEQ, NH, 128, NKB * 4, 128], BF16, kind="Internal").ap()

    P = Prog(nc, es)
    pe, act, dve, pool, sp = P.pe, P.act, P.dve, P.pool, P.sp

    def sb(name, shape, dt):
        return es.enter_context(nc.sbuf_tensor(name, list(shape), dt))

    xin = sb("xin", [128, NCH, T], F32)
    X = sb("X", [128, NCH, T], F32)
    U = sb("U", [128, NCH, T], F32)
    H = sb("H", [128, NCH, T], BF16)
    A = sb("A", [128, 32, T], BF16)
    WR_N = 4
    WR = sb("WR", [128, WR_N, 4096], BF16)
    UB = sb("UB", [128, 3, T], BF16)
    USQ = sb("USQ", [128, 3, T], BF16)
    RT = sb("RT", [128, 3, T], F32)
    ST = sb("ST", [128, 6, T], F32)
    TR = sb("TR", [128, 4, T], F32)
    CS = sb("CS", [128, 2, T], F32)
    POSI = sb("POSI", [128, T], I32)
    KI = POSI
    QRz = sb("QRz", [128, NH, T], BF16)
    CQN = sb("CQN", [128, 4, T], BF16)
    CKVN = sb("CKVN", [128, 2, T], BF16)
    KRS = sb("KRS", [128, S], BF16)
    KB_N = 6
    KBLK = sb("KBLK", [128, KB_N, T], BF16)
    VBLK = sb("VBLK", [128, KB_N, 4, 128], BF16)
    PT_N = 4
    PT = sb("PT", [128, PT_N, T], BF16)
    HALO = sb("HALO", [128, NA, NCH, 16], F32)
    MOD = sb("MOD", [128, DEPTH, 48, NSEQ], F32)
    DRV = sb("DRV", [128, DEPTH, 2, 5, NCH, NSEQ], F32)
    H0C = sb("H0C", [128, 2, NCH, NSEQ], F32)
    LNG = sb("LNG", [128, DEPTH, 2, NCH], F32)
    LNB = sb("LNB", [128, DEPTH, 2, NCH], F32)
    PSC = sb("PSC", [128, NA, NCH], F32)
    QNG = sb("QNG", [128, DEPTH - NA, 4], F32)
    KVG = sb("KVG", [128, 2], F32)
    CST = sb("CST", [128, 32], F32)
    ADAB = sb("ADAB", [128, DEPTH, 48], F32)
    CTs = sb("CTs", [128, NCH, NSEQ], F32)
    SC = sb("SC", [128, NCH, NSEQ], F32)
    TRI = sb("TRI", [128, 128], BF16)
    ONES = sb("ONES", [128, 128], BF16)

    Aflat = A[:, :, :]
    QT = A
    A32 = A[:, :, :].rearrange("p c t -> p (c t)").bitcast(F32)
    ADAT = A32.rearrange("p (s k c) -> p s k c", s=2, k=8)
    OPT_POOLU = False
    if OPT_POOLU:
        HW = A32[:, 0:NCH * 768].rearrange("p (m t) -> p m t", t=768)
        PTMP = U[:, :, :].rearrange("p (k two) t -> p k (two t)", two=2)
    else:
        HW = A32[:, 0:NCH * 528].rearrange("p (m t) -> p m t", t=528)
        PTMP = A32[:, 17 * 256:17 * 256 + 4 * 528].rearrange("p (k t) -> p k t", t=528)
    C32 = A32[:, 0:4 * T].rearrange("p (m t) -> p m t", t=T)

    PS = [es.enter_context(nc.psum_tensor(f"ps{i}", [128, T], F32)) for i in range(8)]
    psb = [Buf(f"ps{i}") for i in range(8)]
    rot_state = [0]

    def rot():
        i = rot_state[0] % 4
        rot_state[0] += 1
        return i

    def bl(name, n):
        return [Buf(f"{name}{i}") for i in range(n)]

    b_xin, b_X, b_U, b_H = bl("xin", NCH), bl("X", NCH), bl("U", NCH), bl("H", NCH)
    b_A = bl("A", 32)
    b_WR = bl("WR", WR_N)
    b_UB, b_USQ, b_RT = bl("UB", 3), bl("USQ", 3), bl("RT", 3)
    b_ST, b_TR = bl("ST", 6), bl("TR", 4)
    b_CS = Buf("CS")
    b_POSI = Buf("POSI")
    b_KI = b_POSI
    b_QR, b_CQN, b_CKVN = bl("QR", NH), bl("CQN", 4), bl("CKVN", 2)
    b_ones = Buf("ones")
    b_KRS = bl("KRS", NKB)
    b_KBLK, b_VBLK = bl("KBLK", KB_N), bl("VBLK", KB_N)
    b_PT = bl("PT", PT_N)
    b_HALO = [bl(f"HALO{l}_", NCH) for l in range(NA)]
    def a_cover(b0, b1):
        return b_A[b0 // 1024:(b1 - 1) // 1024 + 1]

    if OPT_POOLU:
        b_PTMP = [[b_U[2 * k], b_U[2 * k + 1]] for k in range(4)]
        b_HW = [b_A[3 * m:3 * m + 3] for m in range(NCH)]
    else:
        b_PTMP = [a_cover(17408 + k * 2112, 17408 + (k + 1) * 2112) for k in range(4)]
        b_HW = [a_cover(m * 2112, (m + 1) * 2112) for m in range(NCH)]
    b_C32 = [b_A[2 * m:2 * m + 2] for m in range(4)]
    b_ADAT = [b_A[0:16], b_A[16:32]]
    b_const = Buf("const")
    b_MOD = Buf("MOD")
    b_wimg = Buf("wimg")
    b_kd = [bl(f"kd{s}_", NKB) for s in range(NSEQ)]
    b_vd = [bl(f"vd{s}_", NKB) for s in range(NSEQ)]
    b_out = Buf("out")
    b_TMPS = Buf("TMPS")

    c_const = P.new_ctr("const")
    c_cvt = P.new_ctr("cvt")
    c_ada = [P.new_ctr(f"ada{i}") for i in range(2)]
    c_wr = [P.new_ctr(f"wr{i}") for i in range(WR_N)]
    c_xin = P.new_ctr("xin")
    c_out = P.new_ctr("out")
    c_kst = P.new_ctr("kst")
    c_vst = P.new_ctr("vst")
    c_kb = [P.new_ctr(f"kb{i}") for i in range(KB_N)]
    c_vb = [P.new_ctr(f"vb{i}") for i in range(KB_N)]
    c_pos = P.new_ctr("pos")

    c_dbg = P.new_ctr("dbg")

    def checkpoint(label, ap=None, bufs=()):
        if stop != label:
            return
        if ap is not None:
            n = ap.shape[1]
            P.dma(sp, c_dbg, dbg_d[:, 0:n], ap, reads=list(bufs))
        P.stopped = True

    w32v = wimg32.rearrange("n p (a c) -> (n p a) c", c=2048)
    wbv = wimg.rearrange("n p (a c) -> (n p a) c", c=2048)
    rows_total = NT_W * 128 * 2
    RCH = 1024
    for r0 in range(0, rows_total, RCH):
        r1 = min(rows_total, r0 + RCH)
        P.dma(pool, c_cvt, wbv[r0:r1, :], w32v[r0:r1, :], writes=[b_wimg])
    b_wimg.lw = (c_cvt, c_cvt.count)

    for dst, src in ((CTs, cT_d), (ADAB, adab_d), (LNG, lng_d), (LNB, lnb_d), (PSC, psc_d),
                     (QNG, qng_d), (KVG, kvg_d), (CST, cst_d), (TRI, tri_d)):
        nd = len(dst.shape)
        sl = tuple([slice(None)] * nd)
        P.dma(sp, c_const, dst[sl], src[sl], writes=[b_const])
    b_const.lw = (c_const, c_const.count)
    P.op(dve, lambda: nc.vector.memset(ONES[:, :], 1.0), writes=[b_ones])
    P.op(pool, lambda: nc.gpsimd.memset(QRz[:, :, :], 0.0), writes=b_QR)
    P.op(dve, lambda: nc.vector.memset(HALO[:, :, :, :], 0.0), writes=[b for l in b_HALO for b in l])
    P.op(act, lambda: nc.scalar.activation(out=SC[:, :, :], in_=CTs[:, :, :], func=AF.Silu),
         reads=[b_const], writes=[b_MOD])

    for l in range(DEPTH):
        bank = 4 + (l % 2)
        for g in range(12):
            slot = (l * 12 + g) % 2
            P.dma(sp, c_ada[slot], ADAT[:, slot, :, :], adaw_d[l, g, :, :, :], writes=[b_ADAT[slot]])
            for j in range(4):
                m = g * 4 + j
                for kc in range(8):
                    P.op(pe, lambda: nc.tensor.matmul(
                        PS[bank][:, m * NSEQ:(m + 1) * NSEQ], lhsT=ADAT[:, slot, kc, j * 128:(j + 1) * 128],
                        rhs=SC[:, kc, :], start=(kc == 0), stop=(kc == 7)),
                        reads=[b_ADAT[slot], b_MOD], writes=[psb[bank]], signal=(kc == 7))
        for s in range(NSEQ):
            pv = PS[bank][:, 0:48 * NSEQ].rearrange("p (m s) -> p m s", s=NSEQ)
            P.op(dve, lambda: nc.vector.tensor_tensor(out=MOD[:, l, :, s], in0=pv[:, :, s], in1=ADAB[:, l, :],
                                                      op=ALU.add),
                 reads=[psb[bank], b_const], writes=[b_MOD])

    checkpoint("cvt")
    checkpoint("mod", MOD[:, :, :, :].rearrange("p l m s -> p (l m s)"), [b_MOD])
    def modv(l, j, k, s):
        o = (j * 3 + k) * NCH
        return MOD[:, l, o:o + NCH, s]

    for s in range(NSEQ):
        P.op(dve, lambda: nc.vector.tensor_scalar(out=H0C[:, 0, :, s], in0=modv(0, 0, 1, s), scalar1=1.0,
                                                  scalar2=None, op0=ALU.add), reads=[b_MOD], writes=[b_MOD])
        P.op(dve, lambda: nc.vector.tensor_copy(out=H0C[:, 1, :, s], in_=modv(0, 0, 0, s)),
             reads=[b_MOD], writes=[b_MOD])
        for l in range(DEPTH):
            for j in range(2):
                if j == 0 and l < NA:
                    P.op(dve, lambda: nc.vector.scalar_tensor_tensor(
                        out=DRV[:, l, j, 1, :, s], in0=modv(l, j, 2, s), scalar=1.0 / ALPHA, in1=PSC[:, l, :],
                        op0=ALU.mult, op1=ALU.mult), reads=[b_MOD, b_const], writes=[b_MOD])
                else:
                    P.op(dve, lambda: nc.vector.tensor_scalar(
                        out=DRV[:, l, j, 1, :, s], in0=modv(l, j, 2, s), scalar1=1.0 / ALPHA, scalar2=None,
                        op0=ALU.mult), reads=[b_MOD], writes=[b_MOD])
                if j == 0:
                    l2, j2 = l, 1
                elif l + 1 < DEPTH:
                    l2, j2 = l + 1, 0
                else:
                    l2 = None
                if l2 is not None:
                    P.op(dve, lambda: nc.vector.tensor_scalar(
                        out=DRV[:, l, j, 0, :, s], in0=modv(l2, j2, 1, s), scalar1=1.0, scalar2=None, op0=ALU.add),
                        reads=[b_MOD], writes=[b_MOD])
                    P.op(dve, lambda: nc.vector.tensor_tensor(
                        out=DRV[:, l, j, 2, :, s], in0=LNG[:, l, j, :], in1=DRV[:, l, j, 0, :, s], op=ALU.mult),
                        reads=[b_MOD, b_const], writes=[b_MOD])
                    P.op(dve, lambda: nc.vector.tensor_tensor(
                        out=DRV[:, l, j, 4, :, s], in0=LNB[:, l, j, :], in1=DRV[:, l, j, 0, :, s], op=ALU.mult),
                        reads=[b_MOD, b_const], writes=[b_MOD])
                    P.op(dve, lambda: nc.vector.tensor_tensor(
                        out=DRV[:, l, j, 3, :, s], in0=DRV[:, l, j, 4, :, s], in1=modv(l2, j2, 0, s), op=ALU.add),
                        reads=[b_MOD], writes=[b_MOD])

    checkpoint("drv", DRV[:, :, :, :, :, :].rearrange("p l j k m s -> p (l j k m s)"), [b_MOD])
    wr_state = [0]

    def wload(idx):
        slot = wr_state[0] % WR_N
        wr_state[0] += 1
        P.dma(sp, c_wr[slot], WR[:, slot, :], wimg[idx, :, :], reads=[b_wimg], writes=[b_WR[slot]])
        return slot

    def mm(bank, col0, col1, lhsT, rhs, start, stop, reads, signal=None):
        if signal is None:
            signal = stop
        P.op(pe, lambda: nc.tensor.matmul(PS[bank][:, col0:col1], lhsT=lhsT, rhs=rhs, start=start, stop=stop),
             reads=reads, writes=[psb[bank]], signal=signal)

    ubi = [0]

    def epilogue_chunk(l, j, s, m, bank, xsrc, xsrc_b):
        P.op(dve, lambda: nc.vector.scalar_tensor_tensor(
            out=U[:, m, :], in0=PS[bank][:, :], scalar=DRV[:, l, j, 1, m:m + 1, s], in1=xsrc[:, m, :],
            op0=ALU.mult, op1=ALU.add), reads=[psb[bank], xsrc_b[m], b_MOD], writes=[b_U[m]])

    def stats_chunk(m, first, last):
        i = ubi[0] % 3
        ubi[0] += 1
        P.op(act, lambda: nc.scalar.activation(out=USQ[:, i, :], in_=U[:, m, :], func=AF.Square),
             reads=[b_U[m]], writes=[b_USQ[i]])
        P.op(dve, lambda: nc.vector.tensor_copy(out=UB[:, i, :], in_=U[:, m, :]),
             reads=[b_U[m]], writes=[b_UB[i]])
        mm(4, 0, T, ONES[:, :], UB[:, i, :], first, last, [b_UB[i], b_ones], signal=True)
        mm(5, 0, T, ONES[:, :], USQ[:, i, :], first, last, [b_USQ[i], b_ones], signal=True)

    def rstd_from(bank_s2, n, eps, mean_slot, with_mean):
        inv = 1.0 / n
        if with_mean:
            P.op(dve, lambda: nc.vector.tensor_scalar(out=ST[:, 0, :], in0=PS[4][:, :], scalar1=inv, scalar2=None,
                                                      op0=ALU.mult), reads=[psb[4]], writes=[b_ST[0]])
            P.op(dve, lambda: nc.vector.tensor_tensor(out=ST[:, 3, :], in0=ST[:, 0, :], in1=ST[:, 0, :],
                                                      op=ALU.mult), reads=[b_ST[0]], writes=[b_ST[3]])
            P.op(dve, lambda: nc.vector.scalar_tensor_tensor(
                out=ST[:, 4, :], in0=PS[bank_s2][:, :], scalar=inv, in1=ST[:, 3, :], op0=ALU.mult,
                op1=ALU.subtract), reads=[psb[bank_s2], b_ST[3]], writes=[b_ST[4]])
            P.op(dve, lambda: nc.vector.tensor_scalar(out=ST[:, 4, :], in0=ST[:, 4, :], scalar1=0.0, scalar2=eps,
                                                      op0=ALU.max, op1=ALU.add), reads=[b_ST[4]], writes=[b_ST[4]])
        else:
            P.op(dve, lambda: nc.vector.tensor_scalar(out=ST[:, 4, :], in0=PS[bank_s2][:, :], scalar1=inv,
                                                      scalar2=eps, op0=ALU.mult, op1=ALU.add),
                 reads=[psb[bank_s2]], writes=[b_ST[4]])
        P.op(act, lambda: nc.scalar.activation(out=ST[:, 5, :], in_=ST[:, 4, :], func=AF.Ln),
             reads=[b_ST[4]], writes=[b_ST[5]])
        P.op(act, lambda: nc.scalar.activation(out=ST[:, 1, :], in_=ST[:, 5, :], func=AF.Exp, scale=-0.5),
             reads=[b_ST[5]], writes=[b_ST[1]])
        if with_mean:
            P.op(dve, lambda: nc.vector.scalar_tensor_tensor(
                out=ST[:, 2, :], in0=ST[:, 0, :], scalar=-1.0, in1=ST[:, 1, :], op0=ALU.mult, op1=ALU.mult),
                reads=[b_ST[0], b_ST[1]], writes=[b_ST[2]])

    def ln_finish(l, j, s, last_sub, pool_next):
        rstd_from(5, D, LN_EPS / (ALPHA * ALPHA), 0, True)
        for m in range(NCH):
            P.op(dve, lambda: nc.vector.tensor_tensor(out=U[:, m, :], in0=U[:, m, :], in1=ST[:, 1, :], op=ALU.mult),
                 reads=[b_U[m], b_ST[1]], writes=[b_U[m]])
            P.op(dve, lambda: nc.vector.tensor_tensor(out=U[:, m, :], in0=U[:, m, :], in1=ST[:, 2, :], op=ALU.add),
                 reads=[b_U[m], b_ST[2]], writes=[b_U[m]])
            if not last_sub:
                if pool_next:
                    P.op(act, lambda: nc.scalar.activation(
                        out=HW[:, m, 16:528], in_=U[:, m, :], func=AF.Identity,
                        bias=DRV[:, l, j, 3, m:m + 1, s], scale=DRV[:, l, j, 2, m:m + 1, s]),
                        reads=[b_U[m], b_MOD], writes=[b_HW[m]])
                else:
                    P.op(act, lambda: nc.scalar.activation(
                        out=H[:, m, :], in_=U[:, m, :], func=AF.Identity,
                        bias=DRV[:, l, j, 3, m:m + 1, s], scale=DRV[:, l, j, 2, m:m + 1, s]),
                        reads=[b_U[m], b_MOD], writes=[b_H[m]])
        for m in range(NCH):
            P.op(act, lambda: nc.scalar.activation(out=X[:, m, :], in_=U[:, m, :], func=AF.Identity,
                                                   bias=LNB[:, l, j, m:m + 1], scale=LNG[:, l, j, m:m + 1]),
                 reads=[b_U[m], b_const], writes=[b_X[m]])

    def sublayer_outputs(l, j, s, xsrc, xsrc_b, produce):
        pend = []
        for m in range(NCH):
            bank = produce(m)
            epilogue_chunk(l, j, s, m, bank, xsrc, xsrc_b)
            pend.append(m)
            if len(pend) > 1:
                mmm = pend.pop(0)
                stats_chunk(mmm, mmm == 0, False)
        while pend:
            mmm = pend.pop(0)
            stats_chunk(mmm, mmm == 0, mmm == NCH - 1)

    def pool_mixer(l, s, first_tile, xsrc, xsrc_b):
        wslot = wload(wt_index(l, "pool"))
        Wp = WR[:, wslot, :]
        for m in range(NCH):
            eng = dve if m % 2 == 0 else pool
            P.op(eng, lambda: eng.h.tensor_copy(out=HW[:, m, 0:16], in_=HALO[:, l, m, :]),
                 reads=[b_HALO[l][m]], writes=[b_HW[m]])
        for m in range(NCH):
            g = m // 2
            w = 2 << g
            eng = dve if m in (0, 2, 4) else pool
            t0i = 0 if eng is dve else 2
            src = HW[:, m, 0:528]
            srcb = b_HW[m]
            sh = 1
            lo = 16 - (w - 1)
            k = 0
            while sh < w:
                lo2 = lo + sh
                dst = PTMP[:, t0i + (k % 2), :]
                dstb = b_PTMP[t0i + (k % 2)]
                P.op(eng, lambda: eng.h.tensor_tensor(out=dst[:, lo2:528], in0=src[:, lo2:528],
                                                      in1=src[:, lo2 - sh:528 - sh], op=ALU.add),
                     reads=[srcb], writes=[dstb])
                src, srcb = dst, dstb
                lo = lo2
                sh *= 2
                k += 1
            P.op(dve, lambda: nc.vector.scalar_tensor_tensor(out=H[:, m, :], in0=src[:, 16:528], scalar=1.0 / w,
                                                             in1=HW[:, m, 16:528], op0=ALU.mult, op1=ALU.subtract),
                 reads=[srcb, b_HW[m]], writes=[b_H[m]])
            if first_tile:
                P.op(eng, lambda: eng.h.tensor_tensor(out=src[:, 16:16 + w - 1], in0=src[:, 16:16 + w - 1],
                                                      in1=CST[:, 16:16 + w - 1], op=ALU.mult),
                     reads=[srcb, b_const], writes=[srcb])
                P.op(eng, lambda: eng.h.tensor_tensor(out=H[:, m, 0:w - 1], in0=src[:, 16:16 + w - 1],
                                                      in1=HW[:, m, 16:16 + w - 1], op=ALU.subtract),
                     reads=[srcb, b_HW[m]], writes=[b_H[m]])
            P.op(eng, lambda: eng.h.tensor_copy(out=HALO[:, l, m, :], in_=HW[:, m, 512:528]),
                 reads=[b_HW[m]], writes=[b_HALO[l][m]])

        def produce(m):
            g, mo = m // 2, m % 2
            bank = rot()
            for kc in range(2):
                off = g * 512 + kc * 256 + mo * 128
                mm(bank, 0, T, Wp[:, off:off + 128], H[:, 2 * g + kc, :], kc == 0, kc == 1,
                   [b_WR[wslot], b_H[2 * g + kc]])
            return bank

        sublayer_outputs(l, 0, s, xsrc, xsrc_b, produce)

    def mlp(l, s):
        for g in range(8):
            wslot = wload(wt_index(l, "w1", g))
            if g == 0:
                gbanks = [rot() for _ in range(4)]
                for kc in range(8):
                    for jj in range(4):
                        mm(gbanks[jj], 0, T, WR[:, wslot, kc * 512 + jj * 128: kc * 512 + (jj + 1) * 128],
                           H[:, kc, :], kc == 0, kc == 7, [b_WR[wslot], b_H[kc]])
            for jj in range(4):
                if g == 0:
                    bank = gbanks[jj]
                else:
                    bank = rot()
                    for kc in range(8):
                        mm(bank, 0, T, WR[:, wslot, kc * 512 + jj * 128: kc * 512 + (jj + 1) * 128], H[:, kc, :],
                           kc == 0, kc == 7, [b_WR[wslot], b_H[kc]])
                i = ubi[0] % 3
                ubi[0] += 1
                f = g * 4 + jj
                P.op(act, lambda: nc.scalar.activation(out=RT[:, i, :], in_=PS[bank][:, :], func=AF.Relu),
                     reads=[psb[bank]], writes=[b_RT[i]])
                P.op(dve, lambda: nc.vector.tensor_tensor(out=A[:, f, :], in0=RT[:, i, :], in1=RT[:, i, :],
                                                          op=ALU.mult), reads=[b_RT[i]], writes=[b_A[f]])
        first_stat = [True]
        pend = []

        def flush_stats(n, final):
            k = 0
            while pend and k < n:
                mmm = pend.pop(0)
                stats_chunk(mmm, mmm == 0, mmm == NCH - 1)
                k += 1

        for half in range(2):
            banks = [0, 1, 2, 3] if half == 0 else [6, 7, 2, 3]
            if half == 1:
                banks = [6, 7, 0, 1]
            for jt in range(4):
                wslot = wload(wt_index(l, "w2", half * 4 + jt))
                for fl in range(8):
                    ffc = jt * 8 + fl
                    for q in range(4):
                        mm(banks[q], 0, T, WR[:, wslot, fl * 512 + q * 128: fl * 512 + (q + 1) * 128], A[:, ffc, :],
                           ffc == 0, ffc == 31, [b_WR[wslot], b_A[ffc]], signal=(ffc == 31 or (fl == 7 and q == 3)))
                if half == 1 and jt == 1:
                    flush_stats(4, False)
            for q in range(4):
                m = half * 4 + q
                if half == 1 and q >= 2:
                    i = ubi[0] % 3
                    ubi[0] += 1
                    bq = banks[q]
                    P.op(act, lambda: nc.scalar.activation(out=RT[:, i, :], in_=PS[bq][:, :], func=AF.Identity,
                                                           scale=DRV[:, l, 1, 1, m:m + 1, s]),
                         reads=[psb[bq], b_MOD], writes=[b_RT[i]])
                    P.op(pool, lambda: nc.gpsimd.tensor_tensor(out=U[:, m, :], in0=RT[:, i, :], in1=X[:, m, :],
                                                               op=ALU.add),
                         reads=[b_RT[i], b_X[m]], writes=[b_U[m]])
                else:
                    epilogue_chunk(l, 1, s, m, banks[q], X, b_X)
                pend.append(m)
        flush_stats(8, True)

    def rope_tables(s, t0):
        src = bass.AP(pos_t, s * S + t0, [[0, 128], [1, T]])
        P.dma(sp, c_pos, POSI[:, :], src, writes=[b_POSI])
        P.op(dve, lambda: nc.vector.tensor_copy(out=TR[:, 0, :], in_=POSI[:, :]), reads=[b_POSI], writes=[b_TR[0]])
        P.op(dve, lambda: nc.vector.tensor_scalar(out=TR[:, 0, :], in0=TR[:, 0, :], scalar1=CST[:, 0:1], scalar2=None,
                                                  op0=ALU.mult), reads=[b_TR[0], b_const], writes=[b_TR[0]])
        a_ap, a_b = TR[:, 0, :], b_TR[0]
        P.op(dve, lambda: nc.vector.tensor_scalar(out=KI[:, :], in0=a_ap, scalar1=1.0 / TWO_PI, scalar2=None,
                                                  op0=ALU.mult), reads=[a_b], writes=[b_KI])
        P.op(dve, lambda: nc.vector.tensor_copy(out=TR[:, 2, :], in_=KI[:, :]), reads=[b_KI], writes=[b_TR[2]])
        P.op(dve, lambda: nc.vector.scalar_tensor_tensor(out=TR[:, 3, :], in0=TR[:, 2, :], scalar=-C1, in1=a_ap,
                                                         op0=ALU.mult, op1=ALU.add),
             reads=[b_TR[2], a_b], writes=[b_TR[3]])
        P.op(dve, lambda: nc.vector.scalar_tensor_tensor(out=TR[:, 3, :], in0=TR[:, 2, :], scalar=-C2,
                                                         in1=TR[:, 3, :], op0=ALU.mult, op1=ALU.add),
             reads=[b_TR[2], b_TR[3]], writes=[b_TR[3]])
        LIM = math.pi - 2e-6
        P.op(dve, lambda: nc.vector.tensor_scalar(out=TR[:, 3, :], in0=TR[:, 3, :], scalar1=LIM,
                                                  scalar2=-LIM, op0=ALU.min, op1=ALU.max),
             reads=[b_TR[3]], writes=[b_TR[3]])
        P.op(act, lambda: nc.scalar.activation(out=CS[:, 1, :], in_=TR[:, 3, :], func=AF.Sin,
                                               scale=CST[:, 1:2]), reads=[b_TR[3], b_const], writes=[b_CS])
        P.op(dve, lambda: nc.vector.tensor_scalar(out=TR[:, 1, :], in0=TR[:, 3, :], scalar1=math.pi / 2,
                                                  scalar2=None, op0=ALU.add), reads=[b_TR[3]], writes=[b_TR[1]])
        P.op(dve, lambda: nc.vector.tensor_scalar(out=TR[:, 2, :], in0=TR[:, 1, :], scalar1=math.pi,
                                                  scalar2=-TWO_PI, op0=ALU.is_gt, op1=ALU.mult),
             reads=[b_TR[1]], writes=[b_TR[2]])
        P.op(dve, lambda: nc.vector.tensor_tensor(out=TR[:, 1, :], in0=TR[:, 1, :], in1=TR[:, 2, :], op=ALU.add),
             reads=[b_TR[1], b_TR[2]], writes=[b_TR[1]])
        P.op(dve, lambda: nc.vector.tensor_scalar(out=TR[:, 1, :], in0=TR[:, 1, :], scalar1=LIM,
                                                  scalar2=-LIM, op0=ALU.min, op1=ALU.max),
             reads=[b_TR[1]], writes=[b_TR[1]])
        P.op(act, lambda: nc.scalar.activation(out=CS[:, 0, :], in_=TR[:, 1, :], func=AF.Sin),
             reads=[b_TR[1]], writes=[b_CS])

    def rope_apply(bankA, bankB, out_ap, out_bufs, qpair=None):
        P.op(dve, lambda: nc.vector.tensor_tensor(out=TR[:, 0, :], in0=PS[bankA][:, :], in1=CS[:, 0, :], op=ALU.mult),
             reads=[psb[bankA], b_CS], writes=[b_TR[0]])
        P.op(dve, lambda: nc.vector.tensor_tensor(out=TR[:, 1, :], in0=PS[bankB][:, :], in1=CS[:, 1, :], op=ALU.mult),
             reads=[psb[bankB], b_CS], writes=[b_TR[1]])
        if qpair is None:
            P.op(pool, lambda: nc.gpsimd.tensor_tensor(out=out_ap, in0=TR[:, 0, :], in1=TR[:, 1, :], op=ALU.add),
                 reads=[b_TR[0], b_TR[1]], writes=out_bufs)
        else:
            for hh in range(2):
                p0 = hh * 64
                P.op(pool, lambda: nc.gpsimd.tensor_tensor(out=QRz[p0:p0 + 64, 2 * qpair + hh, :],
                                                           in0=TR[p0:p0 + 64, 0, :], in1=TR[p0:p0 + 64, 1, :],
                                                           op=ALU.add),
                     reads=[b_TR[0], b_TR[1]], writes=[b_QR[2 * qpair + hh]])

    def rms_chunks(nch, srcs_bank_fn, g_ap_fn, out_ap_fn, out_bufs, eps, n):
        for mc in range(nch):
            bank = srcs_bank_fn(mc)
            P.op(act, lambda: nc.scalar.copy(out=C32[:, mc, :], in_=PS[bank][:, :]),
                 reads=[psb[bank]], writes=[b_C32[mc]])
            i = ubi[0] % 3
            ubi[0] += 1
            P.op(act, lambda: nc.scalar.activation(out=USQ[:, i, :], in_=C32[:, mc, :], func=AF.Square),
                 reads=[b_C32[mc]], writes=[b_USQ[i]])
            mm(5, 0, T, ONES[:, :], USQ[:, i, :], mc == 0, mc == nch - 1, [b_USQ[i], b_ones], signal=True)
        rstd_from(5, n, eps, 0, False)
        for mc in range(nch):
            P.op(dve, lambda: nc.vector.scalar_tensor_tensor(
                out=out_ap_fn(mc), in0=C32[:, mc, :], scalar=g_ap_fn(mc), in1=ST[:, 1, :], op0=ALU.mult,
                op1=ALU.mult), reads=[b_C32[mc], b_ST[1], b_const], writes=[out_bufs[mc]])

    def kv_phase(s, ti):
        t0 = ti * T
        for m in range(NCH):
            P.op(pool, lambda: nc.gpsimd.tensor_copy(out=A[:, 8 + m, :], in_=X[:, m, :]),
                 reads=[b_X[m]], writes=[b_A[8 + m]])
        ws = wload(wt_index(0, "kvin"))

        pbanks = [rot() for _ in range(4)]
        for kc in range(8):
            for mc in range(4):
                mm(pbanks[mc], 0, T, WR[:, ws, kc * 512 + mc * 128: kc * 512 + (mc + 1) * 128], A[:, 8 + kc, :],
                   kc == 0, kc == 7, [b_WR[ws], b_A[8 + kc]])

        rms_chunks(2, lambda mc: pbanks[mc], lambda mc: KVG[:, mc:mc + 1], lambda mc: CKVN[:, mc, :], b_CKVN,
                   RMS_EPS, KVR)
        bA = pbanks[2]
        bB = pbanks[3]
        rope_apply(bA, bB, KRS[:, t0:t0 + T], [b_KRS[ti]])
        ws2 = wload(wt_index(0, "kvup"))
        for h in range(NH):
            bank = rot()
            for kc in range(2):
                mm(bank, 0, T, WR[:, ws2, kc * 1024 + h * 128: kc * 1024 + (h + 1) * 128], CKVN[:, kc, :],
                   kc == 0, kc == 1, [b_WR[ws2], b_CKVN[kc]])
            eng = act if h % 2 == 0 else dve
            if h % 2 == 0:
                P.op(act, lambda: nc.scalar.copy(out=A[:, 16 + h, :], in_=PS[bank][:, :]),
                     reads=[psb[bank]], writes=[b_A[16 + h]])
            else:
                P.op(dve, lambda: nc.vector.tensor_copy(out=A[:, 16 + h, :], in_=PS[bank][:, :]),
                     reads=[psb[bank]], writes=[b_A[16 + h]])
        P.dma(pool, c_kst, kT_d[s, :, :, t0:t0 + T].rearrange("h p t -> p h t"), A[:, 16:24, :],
              reads=b_A[16:24], writes=[b_kd[s][ti]])
        for ks in range(4):
            for hg in range(2):
                bank = rot()
                for kc in range(2):
                    mm(bank, 0, T, CKVN[:, kc, ks * 128:(ks + 1) * 128],
                       WR[:, ws2, 2048 + kc * 1024 + hg * 512: 2048 + kc * 1024 + (hg + 1) * 512],
                       kc == 0, kc == 1, [b_WR[ws2], b_CKVN[kc]])
                o = A[:, 24 + hg * 4: 24 + hg * 4 + 4, ks * 128:(ks + 1) * 128]
                i_ = PS[bank][:, :].rearrange("p (h d) -> p h d", d=128)
                if (ks + hg) % 2 == 0:
                    P.op(act, lambda: nc.scalar.copy(out=o, in_=i_), reads=[psb[bank]],
                         writes=b_A[24 + hg * 4: 24 + hg * 4 + 4])
                else:
                    P.op(dve, lambda: nc.vector.tensor_copy(out=o, in_=i_), reads=[psb[bank]],
                         writes=b_A[24 + hg * 4: 24 + hg * 4 + 4])
        P.dma(pool, c_vst, v_d[s, :, :, ti * 4:(ti + 1) * 4, :].rearrange("h p k d -> p h (k d)"), A[:, 24:32, :],
              reads=b_A[24:32], writes=[b_vd[s][ti]])

    kb_state = [0]

    def attention(l, s, ti):
        jl = l - NA
        ws = wload(wt_index(l, "qd"))

        qbanks = [rot() for _ in range(4)]
        for kc in range(8):
            for mc in range(4):
                mm(qbanks[mc], 0, T, WR[:, ws, kc * 512 + mc * 128: kc * 512 + (mc + 1) * 128], H[:, kc, :],
                   kc == 0, kc == 7, [b_WR[ws], b_H[kc]])

        rms_chunks(4, lambda mc: qbanks[mc], lambda mc: QNG[:, jl, mc:mc + 1], lambda mc: CQN[:, mc, :], b_CQN, RMS_EPS, QR)
        wa = wload(wt_index(l, "quA"))
        for h in range(NH):
            bank = rot()
            for kc in range(4):
                mm(bank, 0, T, WR[:, wa, kc * 1024 + h * 128: kc * 1024 + (h + 1) * 128], CQN[:, kc, :],
                   kc == 0, kc == 3, [b_WR[wa], b_CQN[kc]])
            if h % 2 == 0:
                P.op(act, lambda: nc.scalar.copy(out=A[:, h, :], in_=PS[bank][:, :]),
                     reads=[psb[bank]], writes=[b_A[h]])
            else:
                P.op(dve, lambda: nc.vector.tensor_copy(out=A[:, h, :], in_=PS[bank][:, :]),
                     reads=[psb[bank]], writes=[b_A[h]])
        wb = wload(wt_index(l, "quB"))
        for pp in range(4):
            bA = rot()
            for kc in range(4):
                mm(bA, 0, T, WR[:, wb, kc * 1024 + pp * 128: kc * 1024 + (pp + 1) * 128], CQN[:, kc, :],
                   kc == 0, kc == 3, [b_WR[wb], b_CQN[kc]])
            bB = rot()
            for kc in range(4):
                mm(bB, 0, T, WR[:, wb, kc * 1024 + 512 + pp * 128: kc * 1024 + 512 + (pp + 1) * 128], CQN[:, kc, :],
                   kc == 0, kc == 3, [b_WR[wb], b_CQN[kc]])
            rope_apply(bA, bB, None, None, qpair=pp)
        subs = []
        for h in range(NH):
            n_units = (ti + 1) * 4
            u_i = 0
            for kb in range(ti + 1):
                for ks in range(4):
                    subs.append(dict(h=h, kb=kb, ks=ks, diag=(kb == ti), first=(u_i == 0), last=(u_i == n_units - 1)))
                    u_i += 1
        slot_of = {}
        pti = [0]
        LA = 2

        def emit_s_exp(sd):
            h, kb, ks = sd["h"], sd["kb"], sd["ks"]
            if ks == 0:
                slot = kb_state[0] % KB_N
                kb_state[0] += 1
                slot_of[(h, kb)] = slot
                P.dma(sp, c_kb[slot], KBLK[:, slot, :], kT_d[s, h, :, kb * T:(kb + 1) * T],
                      reads=[b_kd[s][kb]], writes=[b_KBLK[slot]])
                P.dma(sp, c_vb[slot], VBLK[:, slot, :, :], v_d[s, h, :, kb * 4:(kb + 1) * 4, :],
                      reads=[b_vd[s][kb]], writes=[b_VBLK[slot]])
            slot = slot_of[(h, kb)]
            c0 = ks * 128 if sd["diag"] else 0
            sbank = rot()
            mm(sbank, c0, T, KBLK[:, slot, ks * 128:(ks + 1) * 128], A[:, h, c0:T], True, False,
               [b_KBLK[slot], b_A[h]], signal=False)
            kcol = kb * T + ks * 128
            mm(sbank, c0, T, KRS[:, kcol:kcol + 128], QRz[:, h, c0:T], False, True,
               [b_KRS[kb], b_QR[h]])
            pi = pti[0] % PT_N
            pti[0] += 1
            sd["pi"] = pi
            sd["c0"] = c0
            sd["slot"] = slot
            P.op(act, lambda: nc.scalar.activation(out=PT[:, pi, c0:T], in_=PS[sbank][:, c0:T], func=AF.Exp,
                                                   scale=ATTN_SCALE),
                 reads=[psb[sbank]], writes=[b_PT[pi]])
            if sd["diag"]:
                P.op(pool, lambda: nc.gpsimd.tensor_tensor(out=PT[:, pi, c0:c0 + 128],
                                                           in0=PT[:, pi, c0:c0 + 128], in1=TRI[:, :],
                                                           op=ALU.mult),
                     reads=[b_PT[pi], b_const], writes=[b_PT[pi]])

        def emit_pv(sd):
            h, ks, pi, c0, slot = sd["h"], sd["ks"], sd["pi"], sd["c0"], sd["slot"]
            ob = 4 + (h % 2)
            lb = 6 + (h % 2)
            mm(ob, c0, T, VBLK[:, slot, ks, :], PT[:, pi, c0:T], sd["first"], sd["last"], [b_VBLK[slot], b_PT[pi]])
            mm(lb, c0, T, ONES[:, :], PT[:, pi, c0:T], sd["first"], sd["last"], [b_PT[pi], b_ones], signal=True)
            if sd["last"]:
                si = 3 + (h % 2)
                P.op(dve, lambda: nc.vector.reciprocal(out=ST[:, si, :], in_=PS[lb][:, :]),
                     reads=[psb[lb]], writes=[b_ST[si]])
                P.op(dve, lambda: nc.vector.tensor_tensor(out=A[:, 8 + h, :], in0=PS[ob][:, :], in1=ST[:, si, :],
                                                          op=ALU.mult),
                     reads=[psb[ob], b_ST[si]], writes=[b_A[8 + h]])

        for j in range(len(subs) + LA):
            if j < len(subs):
                emit_s_exp(subs[j])
            if j >= LA:
                emit_pv(subs[j - LA])
        wos = [None, None]

        def produce(m):
            half = m // 4
            if wos[half] is None:
                wos[half] = wload(wt_index(l, "wo", half))
            wsl = wos[half]
            bank = rot()
            for h in range(NH):
                mm(bank, 0, T, WR[:, wsl, h * 512 + (m % 4) * 128: h * 512 + (m % 4 + 1) * 128], A[:, 8 + h, :],
                   h == 0, h == NH - 1, [b_WR[wsl], b_A[8 + h]])
            return bank

        sublayer_outputs(l, 0, s, X, b_X, produce)

    order = [(s, ti) for s in range(NSEQ) for ti in range(NTILE)]

    def load_x(s, ti):
        P.dma(sp, c_xin, xin[:, :, :], xT[s, :, ti * T:(ti + 1) * T].rearrange("(c p) t -> p c t", p=128),
              writes=b_xin)

    load_x(*order[0])
    for oi, (s, ti) in enumerate(order):
        if ti == 0 and s > 0:
            P.op(dve, lambda: nc.vector.memset(HALO[:, :, :, :], 0.0), writes=[b for l in b_HALO for b in l])
        for m in range(NCH):
            P.op(act, lambda: nc.scalar.activation(out=HW[:, m, 16:528], in_=xin[:, m, :], func=AF.Identity,
                                                   bias=H0C[:, 1, m:m + 1, s], scale=H0C[:, 0, m:m + 1, s]),
                 reads=[b_xin[m], b_MOD], writes=[b_HW[m]])
        for l in range(DEPTH):
            xsrc, xsrc_b = (xin, b_xin) if l == 0 else (X, b_X)
            if l < NA:
                pool_mixer(l, s, ti == 0, xsrc, xsrc_b)
            else:
                attention(l, s, ti)
            ln_finish(l, 0, s, False, False)
            if oi == 0:
                checkpoint(f"x{l}0", X[:, :, :].rearrange("p c t -> p (c t)"), b_X)
            if l == 0 and oi + 1 < len(order):
                load_x(*order[oi + 1])
            if l == 0:
                rope_tables(s, ti * T)
                if oi == 0:
                    checkpoint("rope", CS[:, :, :].rearrange("p a t -> p (a t)"), [b_CS])
            mlp(l, s)
            last = l == DEPTH - 1
            ln_finish(l, 1, s, last, (l + 1) < NA)
            if oi == 0:
                checkpoint(f"x{l}1", X[:, :, :].rearrange("p c t -> p (c t)"), b_X)
            if l == NA - 1:
                kv_phase(s, ti)
                if oi == 0:
                    checkpoint("kv")
        P.dma(pool, c_out, outT[s, :, ti * T:(ti + 1) * T].rearrange("(c p) t -> p c t", p=128), X[:, :, :],
              reads=b_X, writes=[b_out])

    assert P.check_deadlock(), "deadlock in generated program"
    P.wait_all(pool, [c_out, c_kst, c_vst, c_cvt])
    P.wait_all(sp, c_kb + c_vb + c_wr + [c_xin, c_pos, c_const, c_dbg] + c_ada)
    es.close()
    return nc


def make_core_inputs(inp, seqs, S):
    x = inp["x"][seqs, :S]
    m = {}
    m["xT"] = np.ascontiguousarray(np.transpose(x, (0, 2, 1)))
    m["pos"] = np.ascontiguousarray(inp["positions"][seqs, :S]).astype(np.int32)
    m["cT"] = np.ascontiguousarray(np.transpose(fm(inp["c"][seqs]), (0, 2, 1)))
    return m


def make_shared_inputs(inp):
    m = {}
    m["adaw"] = build_ada_image(np.asarray(inp["ada_w"], np.float32))
    m["adab"] = fm(inp["ada_b"])
    m["lng"] = fm(inp["ln_g"])
    m["lnb"] = fm(inp["ln_b"])
    m["psc"] = fm(inp["pool_scale"])
    m["qng"] = fm(inp["q_norm_g"])
    m["kvg"] = fm(inp["kv_norm_g"])
    cst = np.zeros((128, 32), np.float32)
    inv_freq = (10000.0 ** (-np.arange(0, 64, 2, dtype=np.float32) / np.float32(64))).astype(np.float32)
    p = np.arange(128)
    cst[:, 0] = inv_freq[p % 32]
    cst[:, 1] = np.where((p % 64) < 32, -1.0, 1.0)
    cst[:, 16:32] = 1.0 / (np.arange(16, dtype=np.float32) + 1.0)
    m["cst"] = cst
    k = np.arange(128)[:, None]
    q = np.arange(128)[None, :]
    m["tri"] = (k <= q).astype(np.float32).astype(ml_dtypes.bfloat16)
    m["wimg32"] = build_weight_image(inp)
    return m


_NC_CACHE = {}


def run(inp, n_cores, NSEQ, S, trace=False, stop=None):
    inp = {k: np.asarray(v) for k, v in inp.items()}
    shared = make_shared_inputs(inp)
    in_maps = []
    for c in range(n_cores):
        seqs = list(range(c * NSEQ, (c + 1) * NSEQ))
        m = dict(shared)
        m.update(make_core_inputs(inp, seqs, S))
        in_maps.append(m)
    key = (NSEQ, S)
    nc = build_program(NSEQ, S, stop)
    res = run_bass_kernel_spmd(nc, in_maps, core_ids=list(range(n_cores)), trace=trace)
    outs = [np.transpose(r["outT"], (0, 2, 1)) for r in res.results]
    return np.ascontiguousarray(np.concatenate(outs, axis=0)).astype(np.float32), res


def kernel(**inputs):
    out, _ = run(inputs, 8, 2, 4096)
    return out
```
